# Optimizing a Trainium2 kernel written in Bass

```python
import math
import jax
import jax.numpy as jnp
from jax import lax
import numpy as np

D_MODEL = 1024
BATCH = 4
SEQ = 8192
DEPTH = 4

HEAD_DIM = 64
DA_HEADS = 8
DA_VDIM = 2 * HEAD_DIM
SW_Q_HEADS = 16
SW_KV_HEADS = 4
SW_GROUP = SW_Q_HEADS // SW_KV_HEADS
WINDOW = 128
Q_BLOCK = 128
D_FF = 2816
CONV_W = 3
ROPE_THETA = 10000.0
NORM_EPS = 1e-6
SUBLN_EPS = 1e-5
N_BRANCH = 2
N_MOD = 6

DA_QK = DA_HEADS * 2 * HEAD_DIM
DA_V = DA_HEADS * DA_VDIM
SW_Q = SW_Q_HEADS * HEAD_DIM
SW_KV = SW_KV_HEADS * HEAD_DIM
IN_SIZES = (DA_QK, DA_QK, DA_V, SW_Q, SW_KV, SW_KV, N_BRANCH * D_MODEL)
IN_WIDTH = sum(IN_SIZES)
IN_SPLITS = tuple(int(s) for s in np.cumsum(IN_SIZES)[:-1])

kernel_name = 'hybrid_diffattn_swa_sinks_convffn'


def rmsnorm(x, g, eps=NORM_EPS):
    xf = x.astype(jnp.float32)
    y = xf * lax.rsqrt(jnp.mean(xf * xf, axis=-1, keepdims=True) + eps)
    return y.astype(x.dtype) * g


def rope_tables(positions):
    inv_freq = ROPE_THETA ** (-jnp.arange(0, HEAD_DIM, 2, dtype=jnp.float32) / HEAD_DIM)
    ang = positions.astype(jnp.float32)[..., None] * inv_freq
    return jnp.cos(ang), jnp.sin(ang)


def apply_rope(x, cos, sin):
    extra = x.ndim - 3
    shp = cos.shape[:2] + (1,) * extra + cos.shape[2:]
    cos = cos.reshape(shp).astype(x.dtype)
    sin = sin.reshape(shp).astype(x.dtype)
    x1, x2 = jnp.split(x, 2, axis=-1)
    return jnp.concatenate([x1 * cos - x2 * sin, x2 * cos + x1 * sin], axis=-1)


def diff_attention(q, k, v, lam, subln_g, lambda_init):
    B, S = q.shape[0], q.shape[1]
    nblk = S // Q_BLOCK
    scale = HEAD_DIM ** -0.5
    qb = q.reshape(B, nblk, Q_BLOCK, DA_HEADS, 2, HEAD_DIM).transpose(1, 0, 2, 3, 4, 5)
    kpos = jnp.arange(S)

    def one_block(args):
        qi, blk = args
        s = jnp.einsum('bqhcd,bkhcd->bhcqk', qi, k).astype(jnp.float32) * scale
        qpos = blk * Q_BLOCK + jnp.arange(Q_BLOCK)
        causal = kpos[None, :] <= qpos[:, None]
        s = jnp.where(causal, s, -jnp.inf)
        p = jax.nn.softmax(s, axis=-1)
        w = p[:, :, 0] - lam * p[:, :, 1]
        return jnp.einsum('bhqk,bkhe->bqhe', w.astype(v.dtype), v)

    o = lax.map(one_block, (qb, jnp.arange(nblk)))
    o = o.transpose(1, 0, 2, 3, 4).reshape(B, S, DA_HEADS, DA_VDIM)
    o = rmsnorm(o, subln_g, SUBLN_EPS) * (1.0 - lambda_init)
    return o.reshape(B, S, DA_V)


def sliding_window_attention(q, k, v, sinks):
    B, S = q.shape[0], q.shape[1]
    nblk = S // Q_BLOCK
    scale = HEAD_DIM ** -0.5
    qb = q.reshape(B, nblk, Q_BLOCK, SW_KV_HEADS, SW_GROUP, HEAD_DIM)
    pad = jnp.zeros((B, Q_BLOCK, SW_KV_HEADS, HEAD_DIM), k.dtype)
    kp = jnp.concatenate([pad, k], axis=1).reshape(B, nblk + 1, Q_BLOCK, SW_KV_HEADS, HEAD_DIM)
    vp = jnp.concatenate([pad, v], axis=1).reshape(B, nblk + 1, Q_BLOCK, SW_KV_HEADS, HEAD_DIM)
    kb = jnp.concatenate([kp[:, :-1], kp[:, 1:]], axis=2)
    vb = jnp.concatenate([vp[:, :-1], vp[:, 1:]], axis=2)
    s = jnp.einsum('bnqhgd,bnkhd->bnhgqk', qb, kb).astype(jnp.float32) * scale
    i = jnp.arange(Q_BLOCK)[:, None]
    j = jnp.arange(2 * Q_BLOCK)[None, :]
    band = (j > i) & (j <= i + WINDOW)
    kabs = jnp.arange(nblk)[:, None, None] * Q_BLOCK + j[None] - Q_BLOCK
    mask = band[None] & (kabs >= 0)
    s = jnp.where(mask[None, :, None, None], s, -jnp.inf)
    sink = jnp.broadcast_to(sinks.astype(jnp.float32).reshape(1, 1, SW_KV_HEADS, SW_GROUP, 1, 1),
                            s.shape[:-1] + (1,))
    p = jax.nn.softmax(jnp.concatenate([s, sink], axis=-1), axis=-1)[..., :-1]
    o = jnp.einsum('bnhgqk,bnkhd->bnqhgd', p.astype(v.dtype), vb)
    return o.reshape(B, S, SW_Q)


def conv_ffn(h, w_gate, w_up, conv_w, conv_b, w_down):
    S = h.shape[1]
    g = h @ w_gate
    u = h @ w_up
    gp = jnp.pad(g, ((0, 0), (CONV_W - 1, 0), (0, 0)))
    gc = conv_b + conv_w[0] * gp[:, 0:S]
    for t in range(1, CONV_W):
        gc = gc + conv_w[t] * gp[:, t:t + S]
    return (jax.nn.silu(gc) * u) @ w_down


def setup_inputs(seed: int = 0) -> dict:
    key = jax.random.key(seed)
    ks = jax.random.split(key, 20)
    f32 = jnp.float32
    nrm = lambda k, shape, s: (jax.random.normal(k, shape, f32) * s).astype(f32)
    x = nrm(ks[0], (BATCH, SEQ, D_MODEL), 1.0)
    c = nrm(ks[1], (BATCH, D_MODEL), 1.0)
    offset = jax.random.randint(ks[2], (BATCH, 1), 0, 4096, dtype=jnp.int32)
    positions = (offset + jnp.arange(SEQ, dtype=jnp.int32)[None, :]).astype(jnp.int32)
    return {
        'x': x,
        'c': c,
        'positions': positions,
        'w_ada': nrm(ks[3], (DEPTH, D_MODEL, N_MOD * D_MODEL), 0.5 * D_MODEL ** -0.5),
        'b_ada': nrm(ks[4], (DEPTH, N_MOD * D_MODEL), 0.02),
        'attn_norm': 1.0 + nrm(ks[5], (DEPTH, D_MODEL), 0.02),
        'w_in': nrm(ks[6], (DEPTH, D_MODEL, IN_WIDTH), D_MODEL ** -0.5),
        'lambda_qk': nrm(ks[7], (DEPTH, 4, HEAD_DIM), 0.1),
        'subln_norm': 1.0 + nrm(ks[8], (DEPTH, DA_VDIM), 0.02),
        'sinks': nrm(ks[9], (DEPTH, SW_Q_HEADS), 0.5),
        'w_branch_a': nrm(ks[10], (DEPTH, DA_V, D_MODEL), DA_V ** -0.5),
        'w_branch_b': nrm(ks[11], (DEPTH, SW_Q, D_MODEL), SW_Q ** -0.5),
        'w_out': nrm(ks[12], (DEPTH, D_MODEL, D_MODEL), D_MODEL ** -0.5),
        'ffn_norm': 1.0 + nrm(ks[13], (DEPTH, D_MODEL), 0.02),
        'w_gate': nrm(ks[14], (DEPTH, D_MODEL, D_FF), D_MODEL ** -0.5),
        'w_up': nrm(ks[15], (DEPTH, D_MODEL, D_FF), D_MODEL ** -0.5),
        'conv_w': nrm(ks[16], (DEPTH, CONV_W, D_FF), CONV_W ** -0.5),
        'conv_b': nrm(ks[17], (DEPTH, D_FF), 0.02),
        'w_down': nrm(ks[18], (DEPTH, D_FF, D_MODEL), D_FF ** -0.5),
        'final_norm': 1.0 + nrm(ks[19], (D_MODEL,), 0.02),
    }


def reference(x, c, positions, w_ada, b_ada, attn_norm, w_in, lambda_qk, subln_norm, sinks,
              w_branch_a, w_branch_b, w_out, ffn_norm, w_gate, w_up, conv_w, conv_b, w_down,
              final_norm):
    B, S = x.shape[0], x.shape[1]
    cos, sin = rope_tables(positions)
    c_act = jax.nn.silu(c)
    for l in range(DEPTH):
        mod = (c_act @ w_ada[l] + b_ada[l])[:, None, :]
        sh1, sc1, g1, sh2, sc2, g2 = jnp.split(mod, N_MOD, axis=-1)

        h = rmsnorm(x, attn_norm[l]) * (1.0 + sc1) + sh1
        proj = h @ w_in[l]
        qa, ka, va, qs, ksw, vsw, gates = jnp.split(proj, IN_SPLITS, axis=-1)

        lambda_init = 0.8 - 0.6 * math.exp(-0.3 * l)
        lq = lambda_qk[l].astype(jnp.float32)
        lam = jnp.exp(jnp.sum(lq[0] * lq[1])) - jnp.exp(jnp.sum(lq[2] * lq[3])) + lambda_init
        qa = apply_rope(qa.reshape(B, S, DA_HEADS, 2, HEAD_DIM), cos, sin)
        ka = apply_rope(ka.reshape(B, S, DA_HEADS, 2, HEAD_DIM), cos, sin)
        va = va.reshape(B, S, DA_HEADS, DA_VDIM)
        oa = diff_attention(qa, ka, va, lam, subln_norm[l], lambda_init)

        qs = apply_rope(qs.reshape(B, S, SW_Q_HEADS, HEAD_DIM), cos, sin)
        qs = qs.reshape(B, S, SW_KV_HEADS, SW_GROUP, HEAD_DIM)
        ksw = apply_rope(ksw.reshape(B, S, SW_KV_HEADS, HEAD_DIM), cos, sin)
        vsw = vsw.reshape(B, S, SW_KV_HEADS, HEAD_DIM)
        ob = sliding_window_attention(qs, ksw, vsw, sinks[l])

        ga, gb = jnp.split(jax.nn.sigmoid(gates), N_BRANCH, axis=-1)
        mixed = ga * (oa @ w_branch_a[l]) + gb * (ob @ w_branch_b[l])
        x = x + g1 * (mixed @ w_out[l])

        h = rmsnorm(x, ffn_norm[l]) * (1.0 + sc2) + sh2
        x = x + g2 * conv_ffn(h, w_gate[l], w_up[l], conv_w[l], conv_b[l], w_down[l])
    return rmsnorm(x, final_norm)
```

```python
import numpy as np
import ml_dtypes
from contextlib import ExitStack
import concourse.bass as bass
import concourse.mybir as mybir
from concourse.bass_utils import run_bass_kernel_spmd

F32 = mybir.dt.float32
BF16 = mybir.dt.bfloat16
I32 = mybir.dt.int32
AF = mybir.ActivationFunctionType
ALU = mybir.AluOpType
AX = mybir.AxisListType


class Res:
    __slots__ = ("ap", "w", "r", "dsem", "dcount", "name", "banks")

    def __init__(self, ap, name, banks=()):
        self.ap = ap
        self.name = name
        self.banks = banks
        self.w = None
        self.r = {}
        self.dsem = None
        self.dcount = 0


class Eng:
    def __init__(self, name, sem, self_sync):
        self.name = name
        self.sem = sem
        self.count = 0
        self.known = {}
        self.items = []
        self.self_sync = self_sync


class _Rec:
    def __init__(self):
        self.call = None

    def __getattr__(self, name):
        def f(*a, **k):
            self.call = (name, a, k)
            return self
        return f


class Prog:
    def __init__(self, nc, stack, arena_words, n_dsem=90):
        self.nc = nc
        self.stack = stack
        self.eng = {}
        for n in ("pe", "act", "dve", "pool", "sp"):
            sem = stack.enter_context(nc.semaphore("s_" + n))
            self.eng[n] = Eng(n, sem, n in ("act", "dve", "pool"))
        self.arena = stack.enter_context(nc.sbuf_tensor("arena", [128, arena_words], F32))
        self.arena_words = arena_words
        self.top = 0
        self.marks = []
        self.psum = [stack.enter_context(nc.psum_tensor("psb%d" % i, [128, 1024], F32)) for i in range(4)]
        self.dsem_pool = [stack.enter_context(nc.semaphore("d%d" % i)) for i in range(n_dsem)]
        self.free_dsems = [(sm, 0) for sm in self.dsem_pool]
        self.dres = []
        self.scopes = [[]]
        self.bank = [Res(None, "bank%d" % i) for i in range(8)]
        self.dummy = self.alloc("dummy", [8], F32)

    def alloc(self, name, free_shape, dtype, parts=128):
        esz = 2 if dtype == BF16 else 4
        n = int(np.prod(free_shape))
        words = (n * esz + 3) // 4
        words = (words + 7) // 8 * 8
        assert self.top + words <= self.arena_words, (name, self.top, words, self.arena_words)
        ap = self.arena[0:parts, self.top:self.top + words]
        self.top += words
        if dtype != F32:
            ap = ap.bitcast(dtype)
        ap = ap[:, 0:n]
        if len(free_shape) == 2:
            ap = ap.rearrange("p (a b) -> p a b", b=free_shape[1])
        elif len(free_shape) == 3:
            ap = ap.rearrange("p (a b c) -> p a b c", b=free_shape[1], c=free_shape[2])
        r = Res(ap, name)
        self.scopes[-1].append(r)
        return r

    def mark(self):
        self.marks.append(self.top)
        self.scopes.append([])

    def release(self):
        self.top = self.marks.pop()
        for r in self.scopes.pop():
            if r.dsem is not None:
                self.free_dsems.append((r.dsem, r.dcount))
                self.dres.remove(r)
                r.dsem = None

    def psum_res(self, bank2, name, dtype=F32, lo=0, hi=1024):
        ap = self.psum[bank2][:, lo:hi]
        if dtype != F32:
            ap = ap.bitcast(dtype)
        banks = tuple(range(2 * bank2 + lo // 512, 2 * bank2 + (hi - 1) // 512 + 1))
        return Res(ap, name, banks)

    def _filter(self, E, waits):
        out = {}
        for (s, v) in waits:
            if s is E.sem and not E.self_sync:
                continue
            k = id(s)
            if E.known.get(k, 0) >= v:
                continue
            if k not in out or out[k][1] < v:
                out[k] = (s, v)
        for k, (s, v) in out.items():
            E.known[k] = v
        return list(out.values())

    @staticmethod
    def _addr(res, ev):
        k = id(ev[0])
        if k not in res.r or res.r[k][1] < ev[1]:
            res.r[k] = ev

    def op(self, en, fn, reads=(), writes=(), signal=True):
        E = self.eng[en]
        pb = [self.bank[b] for r in list(reads) + list(writes) for b in r.banks]
        reads = [r for r in reads if not r.banks]
        writes = [w for w in writes if not w.banks] + pb
        waits = []
        for r in reads:
            if r.w is not None:
                waits.append(r.w)
        for w in writes:
            if w.w is not None:
                waits.append(w.w)
            waits.extend(w.r.values())
        if signal:
            E.count += 1
            ev = (E.sem, E.count)
        else:
            ev = (E.sem, E.count + 1)
        rec = _Rec()
        fn(rec)
        E.items.append((self._filter(E, waits), rec.call, True if signal else None))
        for r in reads:
            self._addr(r, ev)
        for w in writes:
            w.w = ev
            w.r = {}

    def dma(self, qn, out_ap, in_ap, res, write, **kw):
        Q = self.eng[qn]
        if res.dsem is None:
            res.dsem, res.dcount = self.free_dsems.pop(0)
            self.dres.append(res)
        waits = []
        if res.w is not None:
            waits.append(res.w)
        if write:
            waits.extend(res.r.values())
        res.dcount += 16
        ev = (res.dsem, res.dcount)
        Q.items.append((self._filter(Q, waits), ("dma_start", (), dict(out=out_ap, in_=in_ap, **kw)), res.dsem))
        if write:
            res.w = ev
            res.r = {}
        else:
            self._addr(res, ev)

    def load(self, qn, res, dram_ap, sub=None, **kw):
        self.dma(qn, res.ap if sub is None else sub, dram_ap, res, True, **kw)

    def store(self, qn, dram_ap, res, sub=None, **kw):
        self.dma(qn, dram_ap, res.ap if sub is None else sub, res, False, **kw)

    def barrier(self):
        D = self.eng["dve"]
        waits = []
        for E in self.eng.values():
            if E.count > 0:
                waits.append((E.sem, E.count))
        for r in self.dres:
            waits.append((r.dsem, r.dcount))
        D.count += 1
        dm = self.dummy.ap
        D.items.append((self._filter(D, waits), ("memset", (dm, 0.0), {}), True))
        ev = (D.sem, D.count)
        for n, E in self.eng.items():
            if n == "dve":
                continue
            w = self._filter(E, [ev])
            if w:
                E.items.append((w, None, None))

    def emit(self, final_waits_engine="sp"):
        nc = self.nc
        E = self.eng[final_waits_engine]
        waits = []
        for X in self.eng.values():
            if X.count > 0:
                waits.append((X.sem, X.count))
        for r in self.dres:
            waits.append((r.dsem, r.dcount))
        w = self._filter(E, waits)
        if w:
            E.items.append((w, None, None))

        def replay(e, E):
            for waits, fn, sig in E.items:
                for (s, v) in waits:
                    e.wait_ge(s, v)
                if fn is not None:
                    ins = getattr(e, fn[0])(*fn[1], **fn[2])
                    if sig is True:
                        ins.then_inc(E.sem, 1)
                    elif sig is not None:
                        ins.then_inc(sig, 16)

        with nc.Block() as block:
            @block.tensor
            def _(e):
                replay(e, self.eng["pe"])

            @block.scalar
            def _(e):
                replay(e, self.eng["act"])

            @block.vector
            def _(e):
                replay(e, self.eng["dve"])

            @block.gpsimd
            def _(e):
                replay(e, self.eng["pool"])

            @block.sync
            def _(e):
                replay(e, self.eng["sp"])


D = 1024
DFF = 2816
NFT = DFF // 128
L_ALL = 4
EPS = 1e-6
SUBLN_EPS = 1e-5
TWO_PI = 2.0 * np.pi
C1 = 6.28125
C2 = TWO_PI - 6.28125
MAGIC = 12582912.0

VP_AN, VP_FN, VP_BA, VP_CW, VP_CB = 0, 8, 16, 64, 130
VP_L = 152
VR_BG = 0
VR_FIN = L_ALL * 2048
VR_SUB = VR_FIN + 1024
VR_SNK = VR_SUB + L_ALL * 128
VR_LQ = VR_SNK + L_ALL * 16
VR_N = VR_LQ + L_ALL * 256
CC_ID, CC_PERM, CC_TRI, CC_SUP, CC_INVF, CC_SIGN = 0, 128, 256, 768, 1280, 1281
CC_N = 1282


def lambda_init(l):
    import math
    return 0.8 - 0.6 * math.exp(-0.3 * l)


def build_program(T, layers, debug=False, stop_after=None):
    NB = T // 128
    NTB = T // 512
    nc = bass.Bass("TRN2", target_bir_lowering=False)
    dt_in = lambda name, shape, dt=F32: nc.dram_tensor(name, shape, dt, kind="ExternalInput").ap()
    okind = "ExternalOutput" if debug else "Internal"
    dt_sc = lambda name, shape, dt: nc.dram_tensor(name, shape, dt, kind=okind).ap()
    x_in = dt_in("x_in", [T, D])
    cvec = dt_in("cvec", [128, 8])
    posr = dt_in("posr", [128, T], I32)
    vecp = dt_in("vecp", [128, L_ALL * VP_L])
    vecr = dt_in("vecr", [128, VR_N])
    consts = dt_in("consts", [128, CC_N])
    w_ada = dt_in("w_ada", [L_ALL, D, 6 * D])
    w_in = dt_in("w_in", [L_ALL, D, 6656])
    w_ba = dt_in("w_ba", [L_ALL, D, D])
    w_bb = dt_in("w_bb", [L_ALL, D, D])
    w_out = dt_in("w_out", [L_ALL, D, D])
    w_gate = dt_in("w_gate", [L_ALL, D, DFF])
    w_up = dt_in("w_up", [L_ALL, D, DFF])
    w_down = dt_in("w_down", [L_ALL, DFF, D])
    out = nc.dram_tensor("out", [T, D], F32, kind="ExternalOutput").ap()
    xs = dt_sc("xs", [T, D], F32)
    cosT = dt_sc("cosT", [128, T], F32)
    sinT = dt_sc("sinT", [128, T], F32)
    qT_da = dt_sc("qT_da", [8, 128, T], BF16)
    kT_da = dt_sc("kT_da", [8, 128, T], BF16)
    v_da = dt_sc("v_da", [T, D], BF16)
    qT_sw = dt_sc("qT_sw", [8, 128, T], BF16)
    kT_sw = dt_sc("kT_sw", [2, 128, T], BF16)
    v_sw = dt_sc("v_sw", [T, 256], BF16)
    oa = dt_sc("oa", [T, D], BF16)
    ob = dt_sc("ob", [T, D], BF16)

    with ExitStack() as st:
        P = Prog(nc, st, 51200)
        cst = P.alloc("cst", [CC_N], F32)
        idb = P.alloc("idb", [128], BF16)
        permb = P.alloc("permb", [128], BF16)
        trib = P.alloc("trib", [512], BF16)
        supb = P.alloc("supb", [512], BF16)
        vp = P.alloc("vp", [L_ALL * VP_L], F32)
        cact = P.alloc("cact", [8], F32)
        onesf = P.alloc("onesf", [128], F32)
        modp = P.alloc("modp", [48], F32)
        a1 = P.alloc("a1", [8], F32)
        a2 = P.alloc("a2", [8], F32)
        g1bc = P.alloc("g1bc", [D], F32)
        g2bc = P.alloc("g2bc", [D], F32)
        lamt = P.alloc("lamt", [4], F32)
        gvec = P.alloc("gvec", [128], F32)
        esink = P.alloc("esink", [16], F32)
        fnbc = P.alloc("fnbc", [D], F32)

        P.load("sp", cst, consts)
        P.load("sp", vp, vecp)
        P.load("sp", cact, cvec)
        P.load("sp", fnbc, vecr[:, VR_FIN:VR_FIN + D])
        P.op("dve", lambda e: e.tensor_copy(idb.ap, cst.ap[:, CC_ID:CC_ID + 128]), [cst], [idb])
        P.op("dve", lambda e: e.tensor_copy(permb.ap, cst.ap[:, CC_PERM:CC_PERM + 128]), [cst], [permb])
        P.op("dve", lambda e: e.tensor_copy(trib.ap, cst.ap[:, CC_TRI:CC_TRI + 512]), [cst], [trib])
        P.op("dve", lambda e: e.tensor_copy(supb.ap, cst.ap[:, CC_SUP:CC_SUP + 512]), [cst], [supb])
        P.op("act", lambda e: e.activation(cact.ap, cact.ap, AF.Silu), [cact], [cact])
        P.op("dve", lambda e: e.memset(onesf.ap, 1.0), [], [onesf])
        invf = cst.ap[:, CC_INVF:CC_INVF + 1]
        sgn = cst.ap[:, CC_SIGN:CC_SIGN + 1]

        P.mark()
        CH = min(T, 2048)
        pi_t = P.alloc("pi_t", [CH], I32)
        ang = P.alloc("ang", [CH], F32)
        kk = P.alloc("kk", [CH], F32)
        rr = P.alloc("rr", [CH], F32)
        so = P.alloc("so", [CH], F32)
        co = P.alloc("co", [CH], F32)
        for ch in range(T // CH):
            sl = slice(ch * CH, (ch + 1) * CH)
            P.load("sp", pi_t, posr[:, sl])
            P.op("dve", lambda e: e.tensor_copy(ang.ap, pi_t.ap), [pi_t], [ang])
            P.op("dve", lambda e: e.tensor_scalar(ang.ap, ang.ap, invf, None, ALU.mult), [ang, cst], [ang])
            P.op("dve", lambda e: e.tensor_scalar(kk.ap, ang.ap, 1.0 / TWO_PI, MAGIC, ALU.mult, ALU.add), [ang], [kk])
            P.op("dve", lambda e: e.tensor_scalar(kk.ap, kk.ap, -MAGIC, None, ALU.add), [kk], [kk])
            P.op("dve", lambda e: e.scalar_tensor_tensor(rr.ap, kk.ap, -C1, ang.ap, ALU.mult, ALU.add), [kk, ang], [rr])
            P.op("dve", lambda e: e.scalar_tensor_tensor(rr.ap, kk.ap, -C2, rr.ap, ALU.mult, ALU.add), [kk, rr], [rr])
            P.op("dve", lambda e: e.tensor_scalar(rr.ap, rr.ap, float(np.pi), float(-np.pi), ALU.min, ALU.max), [rr], [rr])
            P.op("act", lambda e: e.activation(so.ap, rr.ap, AF.Sin, scale=sgn), [rr, cst], [so])
            P.op("dve", lambda e: e.tensor_scalar(kk.ap, rr.ap, -1.0, None, ALU.mult), [rr], [kk])
            P.op("dve", lambda e: e.tensor_tensor(kk.ap, kk.ap, rr.ap, ALU.max), [kk, rr], [kk])
            P.op("dve", lambda e: e.tensor_scalar(kk.ap, kk.ap, -1.0, float(np.pi / 2), ALU.mult, ALU.add), [kk], [kk])
            P.op("act", lambda e: e.activation(co.ap, kk.ap, AF.Sin), [kk], [co])
            P.store("sp", sinT[:, sl], so)
            P.store("sp", cosT[:, sl], co)
        P.barrier()
        P.release()
        if stop_after == 'P0':
            P.emit()
            return nc

        x_src = x_in
        for li, l in enumerate(layers):
            last = (li == len(layers) - 1)
            vpo = l * VP_L
            linit = lambda_init(l)
            P.mark()
            wad = [P.alloc("wad%d" % i, [8, 1024], F32) for i in range(2)]
            bgr = P.alloc("bgr", [2048], F32)
            lqr = P.alloc("lqr", [256], F32)
            tmpv = P.alloc("tmpv", [128], F32)
            cbc = P.alloc("cbc", [8, 128], F32)
            for kc in range(8):
                P.op("dve", lambda e, kc=kc: e.tensor_scalar(cbc.ap[:, kc, :], onesf.ap, cact.ap[:, kc:kc + 1], None, ALU.mult),
                     [onesf, cact], [cbc])
            P.load("sp", bgr, vecr[:, VR_BG + l * 2048: VR_BG + (l + 1) * 2048])
            P.load("sp", lqr, vecr[:, VR_LQ + l * 256: VR_LQ + (l + 1) * 256])
            P.load("sp", gvec, vecr[:, VR_SUB + l * 128: VR_SUB + (l + 1) * 128])
            P.load("sp", esink, vecr[:, VR_SNK + l * 16: VR_SNK + (l + 1) * 16])
            P.op("dve", lambda e: e.tensor_scalar(gvec.ap, gvec.ap, float(1.0 - linit), None, ALU.mult), [gvec], [gvec])
            P.op("act", lambda e: e.activation(esink.ap, esink.ap, AF.Exp), [esink], [esink])
            for i in range(2):
                P.op("dve", lambda e, i=i: e.tensor_tensor(tmpv.ap[:, 0:64], lqr.ap[:, i * 128:i * 128 + 64],
                                                          lqr.ap[:, i * 128 + 64:i * 128 + 128], ALU.mult), [lqr], [tmpv])
                P.op("dve", lambda e, i=i: e.reduce_sum(lamt.ap[:, i:i + 1], tmpv.ap[:, 0:64], axis=AX.X), [tmpv], [lamt])
            P.op("act", lambda e: e.activation(lamt.ap[:, 0:2], lamt.ap[:, 0:2], AF.Exp), [lamt], [lamt])
            P.op("dve", lambda e: e.scalar_tensor_tensor(lamt.ap[:, 2:3], lamt.ap[:, 1:2], float(-linit), lamt.ap[:, 0:1],
                                                         ALU.add, ALU.subtract), [lamt], [lamt])
            nlam = lamt.ap[:, 2:3]
            mps = P.psum_res(0, "mps", F32, 0, 48)
            gpsl = [P.psum_res(1, "gps%d" % n, F32, n * 512, (n + 1) * 512) for n in range(2)]
            for g in range(6):
                wt = wad[g % 2]
                P.load("sp", wt, w_ada[l, :, g * 1024:(g + 1) * 1024].rearrange("(kc p) n -> p kc n", p=128))
                if g in (2, 5):
                    gdst = g1bc if g == 2 else g2bc
                    boff = 0 if g == 2 else 1024
                    for n in range(2):
                        gps = gpsl[n]
                        for kc in range(8):
                            P.op("pe", lambda e, kc=kc, n=n, gps=gps, wt=wt: e.matmul(
                                gps.ap, cbc.ap[:, kc, :], wt.ap[:, kc, n * 512:(n + 1) * 512],
                                start=(kc == 0), stop=(kc == 7)), [cbc, wt], [gps], signal=(kc == 7))
                        P.op("dve", lambda e, n=n, gps=gps, gdst=gdst, boff=boff: e.tensor_tensor(
                            gdst.ap[:, n * 512:(n + 1) * 512], gps.ap, bgr.ap[:, boff + n * 512: boff + (n + 1) * 512], ALU.add),
                            [gps, bgr], [gdst])
                else:
                    for j in range(8):
                        for kc in range(8):
                            P.op("pe", lambda e, kc=kc, j=j, g=g, wt=wt: e.matmul(
                                mps.ap[:, g * 8 + j: g * 8 + j + 1], wt.ap[:, kc, j * 128:(j + 1) * 128],
                                cact.ap[:, kc:kc + 1], start=(kc == 0), stop=(kc == 7)),
                                [wt, cact], [mps], signal=(kc == 7 and j == 7))
            P.op("dve", lambda e: e.tensor_tensor(modp.ap, mps.ap, vp.ap[:, vpo + VP_BA: vpo + VP_BA + 48], ALU.add),
                 [mps, vp], [modp])
            P.op("dve", lambda e: e.scalar_tensor_tensor(a1.ap, modp.ap[:, 8:16], 1.0, vp.ap[:, vpo + VP_AN: vpo + VP_AN + 8],
                                                         ALU.add, ALU.mult), [modp, vp], [a1])
            P.op("dve", lambda e: e.scalar_tensor_tensor(a2.ap, modp.ap[:, 32:40], 1.0, vp.ap[:, vpo + VP_FN: vpo + VP_FN + 8],
                                                         ALU.add, ALU.mult), [modp, vp], [a2])
            P.barrier()
            P.release()
            if stop_after == 'PL':
                P.emit()
                return nc

            def norm_block(xt, ssall, col0, ns=4):
                for s in range(ns):
                    P.op("act", lambda e, s=s: e.activation(junk.ap, xt.ap[:, s, :], AF.Square,
                                                            accum_out=ssall.ap[:, col0 + s: col0 + s + 1]),
                         [xt], [junk, ssall])

            def rstd_from(ssall, n, scale, eps):
                P.op("dve", lambda e: e.tensor_scalar(ssall.ap[:, 0:n], ssall.ap[:, 0:n], scale, eps, ALU.mult, ALU.add),
                     [ssall], [ssall])

            def make_hT(xt, ssall, col0, hT, avec, svec, tb2, ns=4):
                for s in range(ns):
                    P.op("dve", lambda e, s=s: e.tensor_scalar(xn.ap[:, s, :], xt.ap[:, s, :], ssall.ap[:, col0 + s: col0 + s + 1],
                                                               None, ALU.mult), [xt, ssall], [xn])
                for m in range(4):
                    tp = tb2[m % 2]
                    for k2 in range(2):
                        kc = 2 * m + k2
                        for s in range(ns):
                            P.op("pe", lambda e, s=s, kc=kc, k2=k2, tp=tp: e.transpose(
                                tp.ap[:, k2 * 512 + s * 128: k2 * 512 + (s + 1) * 128], xn.ap[:, s, kc * 128:(kc + 1) * 128], idb.ap),
                                [xn, idb], [tp], signal=(k2 == 1 and s == ns - 1))
                    for k2 in range(2):
                        kc = 2 * m + k2
                        P.op("dve", lambda e, kc=kc, k2=k2, tp=tp: e.tensor_scalar(
                            hT[kc].ap, tp.ap[:, k2 * 512: k2 * 512 + ns * 128], avec.ap[:, kc:kc + 1], svec[:, kc:kc + 1], ALU.mult, ALU.add),
                            [tp, avec, modp], [hT[kc]])

            P.mark()
            win = [P.alloc("win%d" % kc, [4608], BF16) for kc in range(8)]
            for kc in range(8):
                P.load("pool", win[kc], w_in[l, kc * 128:(kc + 1) * 128, 0:4608])
            xts = [P.alloc("xt%d" % i, [4, D], F32) for i in range(2)]
            junk = P.alloc("junk", [D], BF16)
            xn = P.alloc("xn", [4, D], BF16)
            ssall = P.alloc("ssall", [40], F32)
            hT = [P.alloc("hT%d" % kc, [512], BF16) for kc in range(8)]
            cs_t = P.alloc("cs_t", [512], F32)
            sn_t = P.alloc("sn_t", [512], F32)
            qbs = [P.alloc("qb%d" % i, [512], BF16) for i in range(2)]
            t1s = [P.alloc("t1%d" % i, [512], F32) for i in range(2)]
            t2s = [P.alloc("t2%d" % i, [512], F32) for i in range(2)]
            rss = [P.alloc("rs%d" % i, [512], BF16) for i in range(3)]
            vt = P.alloc("vt", [4, 1280], BF16)
            tps = [P.psum_res(0, "tpb%d" % i, BF16, i * 512, (i + 1) * 512) for i in range(2)]
            pss = [P.psum_res(1 + i // 2, "ps%d" % i, F32, (i % 2) * 512, (i % 2 + 1) * 512) for i in range(4)]
            rps = [P.psum_res(3, "rp%d" % i, F32, i * 512, (i + 1) * 512) for i in range(2)]
            tiles = ([("qda", i, i * 128) for i in range(8)] + [("kda", i, 1024 + i * 128) for i in range(8)] +
                     [("qsw", i, 3072 + i * 128) for i in range(8)] + [("ksw", i, 4096 + i * 128) for i in range(2)])
            dsts = {"qda": qT_da, "kda": kT_da, "qsw": qT_sw, "ksw": kT_sw}
            cnt = 0
            P.load("sp", xts[0], x_src[0:512, :].rearrange("(s p) d -> p s d", p=128))
            for tb in range(NTB):
                xt = xts[tb % 2]
                tsl = slice(tb * 512, (tb + 1) * 512)
                if tb + 1 < NTB:
                    P.load("sp", xts[(tb + 1) % 2], x_src[(tb + 1) * 512:(tb + 2) * 512, :].rearrange("(s p) d -> p s d", p=128))
                P.load("sp", cs_t, cosT[:, tsl])
                P.load("sp", sn_t, sinT[:, tsl])
                norm_block(xt, ssall, 0)
                rstd_from(ssall, 4, 1.0 / D, EPS)
                P.op("act", lambda e: e.activation(ssall.ap[:, 0:4], ssall.ap[:, 0:4], AF.Sqrt), [ssall], [ssall])
                P.op("dve", lambda e: e.reciprocal(ssall.ap[:, 0:4], ssall.ap[:, 0:4]), [ssall], [ssall])
                make_hT(xt, ssall, 0, hT, a1, modp.ap[:, 0:8], tps)
                def mm_tile(ti, cnt_):
                    c0_ = tiles[ti][2]
                    ps_ = pss[cnt_ % 4]
                    for kc in range(8):
                        P.op("pe", lambda e, kc=kc: e.matmul(ps_.ap, win[kc].ap[:, c0_:c0_ + 128], hT[kc].ap,
                                                             start=(kc == 0), stop=(kc == 7)),
                             [win[kc], hT[kc]], [ps_], signal=(kc == 7))
                mm_tile(0, cnt)
                for ti, (fam, idx, c0) in enumerate(tiles):
                    ps = pss[cnt % 4]
                    rp = rps[cnt % 2]
                    qb = qbs[cnt % 2]
                    t1 = t1s[cnt % 2]
                    t2 = t2s[cnt % 2]
                    rs = rss[cnt % 3]
                    cnt += 1
                    if ti + 1 < len(tiles):
                        mm_tile(ti + 1, cnt)
                    P.op("act", lambda e: e.copy(qb.ap, ps.ap), [ps], [qb])
                    P.op("pe", lambda e: e.matmul(rp.ap, permb.ap, qb.ap, start=True, stop=True), [permb, qb], [rp])
                    P.op("dve", lambda e: e.tensor_tensor(t1.ap, ps.ap, cs_t.ap, ALU.mult), [ps, cs_t], [t1])
                    P.op("dve", lambda e: e.tensor_tensor(t2.ap, rp.ap, sn_t.ap, ALU.mult), [rp, sn_t], [t2])
                    P.op("pool", lambda e: e.tensor_tensor(rs.ap, t1.ap, t2.ap, ALU.add), [t1, t2], [rs])
                    P.store("sp", dsts[fam][idx, :, tsl], rs)
                for s in range(4):
                    for (c0, wdt, o0) in ((2048, 512, 0), (2560, 512, 512), (4352, 256, 1024)):
                        ps = pss[cnt % 4]
                        cnt += 1
                        for kc in range(8):
                            P.op("pe", lambda e, kc=kc, ps=ps, c0=c0, wdt=wdt, s=s: e.matmul(
                                ps.ap[:, 0:wdt], hT[kc].ap[:, s * 128:(s + 1) * 128], win[kc].ap[:, c0:c0 + wdt],
                                start=(kc == 0), stop=(kc == 7)), [win[kc], hT[kc]], [ps], signal=(kc == 7))
                        P.op("act", lambda e, ps=ps, wdt=wdt, o0=o0, s=s: e.copy(vt.ap[:, s, o0:o0 + wdt], ps.ap[:, 0:wdt]), [ps], [vt])
                P.store("sp", v_da[tsl, :].rearrange("(s p) d -> p s d", p=128), vt, sub=vt.ap[:, :, 0:1024])
                P.store("sp", v_sw[tsl, :].rearrange("(s p) d -> p s d", p=128), vt, sub=vt.ap[:, :, 1024:1280])
            P.barrier()
            P.release()
            if stop_after == 'P1':
                P.emit()
                return nc

            P.mark()
            SC = min(T, 1024)
            NQB = SC // 128
            QS = P.alloc("QS", [4, SC], BF16)
            KS = P.alloc("KS", [128 + SC], BF16)
            VS = P.alloc("VS", [NQB + 1, 2, 65], BF16)
            pts = [P.alloc("spt%d" % i, [2, 512], BF16) for i in range(2)]
            obt = P.alloc("obt", [NQB, 2, 256], BF16)
            den = P.alloc("den", [4], F32)
            P.op("pool", lambda e: e.memset(VS.ap, 1.0), [], [VS])
            P.op("pool", lambda e: e.memset(KS.ap, 0.0), [], [KS])
            sps = [P.psum_res(i, "sps%d" % i, F32) for i in range(2)]
            ops_ = [P.psum_res(2, "sop%d" % i, F32, i * 512, i * 512 + 260) for i in range(2)]
            pts.append(P.alloc("spt2", [2, 512], BF16))
            for ch in range(T // SC):
                for kp in range(2):
                    t0 = ch * SC
                    for hh in range(2):
                        g = 2 * kp + hh
                        for j in range(4):
                            P.load("sp", QS, qT_sw[2 * g + j // 2, (j % 2) * 64:(j % 2) * 64 + 64, t0:t0 + SC],
                                   sub=QS.ap[hh * 64:(hh + 1) * 64, j, :])
                    if ch > 0:
                        P.load("sp", KS, kT_sw[kp, :, t0 - 128:t0 + SC])
                        for hh in range(2):
                            P.load("sp", VS, v_sw[t0 - 128:t0 + SC, kp * 128 + hh * 64:kp * 128 + (hh + 1) * 64].rearrange("(b p) d -> p b d", p=128),
                                   sub=VS.ap[:, :, hh, 0:64])
                    else:
                        P.load("sp", KS, kT_sw[kp, :, t0:t0 + SC], sub=KS.ap[:, 128:128 + SC])
                        for hh in range(2):
                            P.load("sp", VS, v_sw[t0:t0 + SC, kp * 128 + hh * 64:kp * 128 + (hh + 1) * 64].rearrange("(b p) d -> p b d", p=128),
                                   sub=VS.ap[:, 1:NQB + 1, hh, 0:64])
                    units = [(hh, i) for hh in range(2) for i in range(NQB)]

                    def sw_qk(u):
                        hh_, i_ = units[u]
                        gi_ = ch * NQB + i_
                        sp__ = sps[u % 2]
                        pl_ = slice(hh_ * 64, (hh_ + 1) * 64)
                        kbs_ = ([0] if gi_ > 0 else []) + [1]
                        for jj in kbs_:
                            P.op("pe", lambda e, jj=jj: e.matmul(
                                sp__.ap[:, jj * 512:(jj + 1) * 512], KS.ap[pl_, (i_ + jj) * 128:(i_ + jj + 1) * 128],
                                QS.ap[pl_, :, i_ * 128:(i_ + 1) * 128], start=True, stop=True), [KS, QS], [sp__])
                    sw_qk(0)
                    for u, (hh, i) in enumerate(units):
                        g = 2 * kp + hh
                        gi = ch * NQB + i
                        sp_ = sps[u % 2]
                        pt = pts[u % 3]
                        op_ = ops_[u % 2]
                        kbs = ([0] if gi > 0 else []) + [1]
                        if u + 1 < len(units):
                            sw_qk(u + 1)
                        lo = kbs[0] * 512
                        P.op("act", lambda e: e.activation(
                            pt.ap.rearrange("p a b -> p (a b)")[:, lo:1024], sp_.ap[:, lo:1024], AF.Exp, scale=0.125),
                            [sp_], [pt])
                        for jj in kbs:
                            msk = supb if jj == 0 else trib
                            P.op("pool", lambda e, jj=jj, msk=msk: e.tensor_tensor(pt.ap[:, jj, :], pt.ap[:, jj, :], msk.ap, ALU.mult),
                                 [pt, msk], [pt])
                        for j in range(4):
                            for n_, jj in enumerate(kbs):
                                P.op("pe", lambda e, j=j, jj=jj, n_=n_: e.matmul(
                                    op_.ap[:, j * 65:(j + 1) * 65], pt.ap[:, jj, j * 128:(j + 1) * 128], VS.ap[:, i + jj, hh, :],
                                    start=(n_ == 0), stop=(n_ == len(kbs) - 1)), [pt, VS], [op_],
                                    signal=(n_ == len(kbs) - 1))
                        opv = op_.ap.rearrange("p (j c) -> p j c", c=65)
                        P.op("dve", lambda e: e.tensor_tensor(den.ap, opv[:, :, 64], esink.ap[:, g * 4:(g + 1) * 4], ALU.add),
                             [op_, esink], [den])
                        P.op("dve", lambda e: e.reciprocal(den.ap, den.ap), [den], [den])
                        for j in range(4):
                            P.op("dve", lambda e, j=j: e.tensor_scalar(
                                obt.ap[:, i, hh, j * 64:(j + 1) * 64], opv[:, j, 0:64], den.ap[:, j:j + 1], None, ALU.mult),
                                [op_, den], [obt])
                    P.store("sp", ob[t0:t0 + SC, kp * 512:(kp + 1) * 512].rearrange("(b p) f -> p b f", p=128), obt,
                            sub=obt.ap.rearrange("p b h d -> p b (h d)"))
            P.barrier()
            P.release()
            if stop_after == 'SW':
                P.emit()
                return nc
            P.mark()
            KTs = [P.alloc("KT%d" % i, [T], BF16) for i in range(2)]
            QTs = [P.alloc("QT%d" % i, [T], BF16) for i in range(2)]
            Vs = [P.alloc("V%d" % i, [NB, 129], BF16) for i in range(2)]
            dpts = [P.alloc("dpt%d" % i, [1024], BF16) for i in range(4)]
            o0 = P.alloc("o0", [4, 129], F32)
            ofin = P.alloc("ofin", [4, 128], BF16)
            rcp = P.alloc("rcp", [4], F32)
            rcp8 = P.alloc("rcp8", [4], F32)
            tda = P.alloc("tda", [128], F32)
            for i in range(2):
                P.op("pool", lambda e, i=i: e.memset(Vs[i].ap, 1.0), [], [Vs[i]])
            dsp = [P.psum_res(i, "dsp%d" % i, F32) for i in range(2)]
            dop = [P.psum_res(2 + q // 2, "dop%d" % q, F32, (q % 2) * 512, (q % 2) * 512 + 129) for q in range(4)]
            pcnt = 0

            def load_head(h):
                P.load("sp", KTs[h % 2], kT_da[h])
                P.load("sp", QTs[h % 2], qT_da[h])
                P.load("sp", Vs[h % 2], v_da[:, h * 128:(h + 1) * 128].rearrange("(b p) e -> p b e", p=128),
                       sub=Vs[h % 2].ap[:, :, 0:128])
            load_head(0)
            o1 = P.alloc("o1", [4, 129], F32)
            pairs = []
            for h in range(8):
                for qg in range(NTB):
                    for c in range(2):
                        nkb = 4 * qg + 4
                        for j0 in range(0, nkb, 2):
                            pairs.append((h, qg, c, j0, j0 == 0 and c == 0 and qg == 0, j0 + 2 >= nkb))

            def emit_qk(i):
                h, qg, c, j0, _, _ = pairs[i]
                KT, QT = KTs[h % 2], QTs[h % 2]
                sp_ = dsp[i % 2]
                pl = slice(c * 64, (c + 1) * 64)
                for jj in range(2):
                    j = j0 + jj
                    col0 = max(j - 4 * qg, 0) * 128
                    P.op("pe", lambda e, jj=jj, j=j, col0=col0: e.matmul(
                        sp_.ap[:, jj * 512 + col0:(jj + 1) * 512], KT.ap[pl, j * 128:(j + 1) * 128],
                        QT.ap[pl, qg * 512 + col0:(qg + 1) * 512], start=True, stop=True),
                        [KT, QT], [sp_], signal=(jj == 1))

            emit_qk(0)
            for i, (h, qg, c, j0, first_of_head, last_of_unit) in enumerate(pairs):
                if first_of_head and h + 1 < 8:
                    load_head(h + 1)
                if i + 1 < len(pairs):
                    emit_qk(i + 1)
                V = Vs[h % 2]
                sp_ = dsp[i % 2]
                pt = dpts[i % 4]
                rs_ = [max(j0 + jj - 4 * qg, 0) for jj in range(2)]
                if rs_[0] == 0 and rs_[1] == 0:
                    P.op("act", lambda e: e.activation(pt.ap, sp_.ap, AF.Exp, scale=0.125), [sp_], [pt])
                else:
                    for jj in range(2):
                        a0 = jj * 512 + rs_[jj] * 128
                        a1_ = (jj + 1) * 512
                        P.op("act", lambda e, a0=a0, a1_=a1_: e.activation(
                            pt.ap[:, a0:a1_], sp_.ap[:, a0:a1_], AF.Exp, scale=0.125), [sp_], [pt])
                for jj in range(2):
                    j = j0 + jj
                    if j >= 4 * qg:
                        a0 = jj * 512 + (j - 4 * qg) * 128
                        P.op("pool", lambda e, a0=a0: e.tensor_tensor(pt.ap[:, a0:a0 + 128], pt.ap[:, a0:a0 + 128],
                                                                    trib.ap[:, 0:128], ALU.mult), [pt, trib], [pt])
                for jj in range(2):
                    j = j0 + jj
                    for qb in range(rs_[jj], 4):
                        stp = (j == 4 * qg + qb)
                        P.op("pe", lambda e, jj=jj, j=j, qb=qb, stp=stp: e.matmul(
                            dop[qb].ap, pt.ap[:, jj * 512 + qb * 128: jj * 512 + (qb + 1) * 128], V.ap[:, j, :],
                            start=(j == 0), stop=stp), [pt, V], [dop[qb]], signal=stp)
                if last_of_unit:
                    if c == 0:
                        for qb in range(4):
                            P.op("dve", lambda e, qb=qb: e.tensor_copy(o0.ap[:, qb, :], dop[qb].ap), [dop[qb]], [o0])
                    else:
                        for qb in range(4):
                            P.op("dve", lambda e, qb=qb: e.tensor_copy(o1.ap[:, qb, :], dop[qb].ap), [dop[qb]], [o1])
                        P.op("dve", lambda e: e.reciprocal(rcp.ap[:, 0:4], o0.ap[:, :, 128]), [o0], [rcp])
                        P.op("dve", lambda e: e.reciprocal(rcp8.ap[:, 0:4], o1.ap[:, :, 128]), [o1], [rcp8])
                        P.op("dve", lambda e: e.tensor_scalar(rcp8.ap[:, 0:4], rcp8.ap[:, 0:4], nlam, None, ALU.mult), [rcp8, lamt], [rcp8])
                        for qb in range(4):
                            P.op("dve", lambda e, qb=qb: e.tensor_scalar(tda.ap, o0.ap[:, qb, 0:128], rcp.ap[:, qb:qb + 1], None, ALU.mult),
                                 [o0, rcp], [tda])
                            P.op("dve", lambda e, qb=qb: e.scalar_tensor_tensor(ofin.ap[:, qb, :], o1.ap[:, qb, 0:128], rcp8.ap[:, qb:qb + 1],
                                                                                tda.ap, ALU.mult, ALU.add), [o1, rcp8, tda], [ofin])
                        P.store("sp", oa[qg * 512:(qg + 1) * 512, h * 128:(h + 1) * 128].rearrange("(s p) e -> p s e", p=128), ofin)
            P.barrier()
            P.release()
            if stop_after == 'DA':
                P.emit()
                return nc

            P.mark()
            NS = 2
            W = NS * 128
            wg = [P.alloc("wg%d" % kc, [2048], BF16) for kc in range(8)]
            wa = [P.alloc("wa%d" % kc, [D], BF16) for kc in range(8)]
            wb = [P.alloc("wb%d" % kc, [D], BF16) for kc in range(8)]
            wo = [P.alloc("wo%d" % kc, [D], BF16) for kc in range(8)]
            stg = [P.alloc("stg%d" % i, [D], F32) for i in range(2)]
            for kc in range(8):
                P.load("pool", wg[kc], w_in[l, kc * 128:(kc + 1) * 128, 4608:6656])
                P.load("pool", wa[kc], w_ba[l, kc * 128:(kc + 1) * 128, :])
                P.load("pool", wb[kc], w_bb[l, kc * 128:(kc + 1) * 128, :])
                sg_ = stg[kc % 2]
                P.load("sp", sg_, w_out[l, kc * 128:(kc + 1) * 128, :])
                P.op("pool", lambda e, kc=kc, sg_=sg_: e.tensor_tensor(wo[kc].ap, sg_.ap, g1bc.ap, ALU.mult), [sg_, g1bc], [wo[kc]])
            P.barrier()
            xt = P.alloc("xt", [NS, D], F32)
            oat = P.alloc("oat", [NS, D], BF16)
            obt3 = P.alloc("obt3", [NS, D], BF16)
            junk = P.alloc("junk", [D], BF16)
            junkf = P.alloc("junkf", [D], F32)
            xn = P.alloc("xn", [NS, D], BF16)
            ssall = P.alloc("ssall", [40], F32)
            hT = [P.alloc("hT%d" % kc, [W], BF16) for kc in range(8)]
            oaT = [P.alloc("oaT%d" % kc, [W], BF16) for kc in range(8)]
            obT = [P.alloc("obT%d" % kc, [W], BF16) for kc in range(8)]
            mixT = [P.alloc("mixT%d" % kc, [W], BF16) for kc in range(8)]
            sas = [P.alloc("sa%d" % i, [W], F32) for i in range(2)]
            sbs = [P.alloc("sb%d" % i, [W], F32) for i in range(2)]
            tps = [P.psum_res(0, "tpb%d" % i, BF16, i * 512, (i + 1) * 512) for i in range(2)]
            pss = [P.psum_res(1 + i // 2, "ps%d" % i, F32, (i % 2) * 512, (i % 2 + 1) * 512) for i in range(6)]
            cnt = 0
            for tb in range(T // W):
                tsl = slice(tb * W, (tb + 1) * W)
                P.load("sp", xt, x_src[tsl, :].rearrange("(s p) d -> p s d", p=128))
                P.load("sp", oat, oa[tsl, :].rearrange("(s p) d -> p s d", p=128))
                P.load("sp", obt3, ob[tsl, :].rearrange("(s p) d -> p s d", p=128))
                norm_block(xt, ssall, 0, NS)
                for s in range(NS):
                    P.op("dve", lambda e, s=s: e.tensor_tensor(junkf.ap, oat.ap[:, s, :], oat.ap[:, s, :], ALU.mult), [oat], [junkf])
                    P.op("dve", lambda e, s=s: e.reduce_sum(ssall.ap[:, NS + s * 8: NS + 8 + s * 8],
                                                            junkf.ap.rearrange("p (h e) -> p h e", e=128), axis=AX.X),
                         [junkf], [ssall])
                P.op("dve", lambda e: e.tensor_scalar(ssall.ap[:, 0:NS], ssall.ap[:, 0:NS], 1.0 / D, EPS, ALU.mult, ALU.add), [ssall], [ssall])
                P.op("dve", lambda e: e.tensor_scalar(ssall.ap[:, NS:9 * NS], ssall.ap[:, NS:9 * NS], 1.0 / 128, SUBLN_EPS, ALU.mult, ALU.add), [ssall], [ssall])
                P.op("act", lambda e: e.activation(ssall.ap[:, 0:9 * NS], ssall.ap[:, 0:9 * NS], AF.Sqrt), [ssall], [ssall])
                P.op("dve", lambda e: e.reciprocal(ssall.ap[:, 0:9 * NS], ssall.ap[:, 0:9 * NS]), [ssall], [ssall])
                make_hT(xt, ssall, 0, hT, a1, modp.ap[:, 0:8], tps, NS)
                for s in range(NS):
                    for h in range(8):
                        P.op("dve", lambda e, s=s, h=h: e.scalar_tensor_tensor(
                            oat.ap[:, s, h * 128:(h + 1) * 128], oat.ap[:, s, h * 128:(h + 1) * 128],
                            ssall.ap[:, NS + s * 8 + h: NS + 1 + s * 8 + h], gvec.ap, ALU.mult, ALU.mult), [oat, ssall, gvec], [oat])
                for (src, dstT) in ((oat, oaT), (obt3, obT)):
                    for m in range(4):
                        tp = tps[m % 2]
                        for k2 in range(2):
                            kc = 2 * m + k2
                            for s in range(NS):
                                P.op("pe", lambda e, s=s, kc=kc, k2=k2, tp=tp, src=src: e.transpose(
                                    tp.ap[:, k2 * 512 + s * 128: k2 * 512 + (s + 1) * 128], src.ap[:, s, kc * 128:(kc + 1) * 128], idb.ap),
                                    [src, idb], [tp], signal=(k2 == 1 and s == NS - 1))
                        for k2 in range(2):
                            kc = 2 * m + k2
                            P.op("act", lambda e, kc=kc, k2=k2, tp=tp, dstT=dstT: e.copy(dstT[kc].ap, tp.ap[:, k2 * 512: k2 * 512 + W]),
                                 [tp], [dstT[kc]])
                for ft in range(8):
                    psA, psB, psGA, psGB = [pss[(cnt + i) % 6] for i in range(4)]
                    cnt += 4
                    sa, sb = sas[ft % 2], sbs[ft % 2]
                    for (ps, wl, c0, rhsT) in ((psA, wa, ft * 128, oaT), (psB, wb, ft * 128, obT),
                                               (psGA, wg, ft * 128, hT), (psGB, wg, 1024 + ft * 128, hT)):
                        for kc in range(8):
                            P.op("pe", lambda e, kc=kc, ps=ps, wl=wl, c0=c0, rhsT=rhsT: e.matmul(
                                ps.ap[:, 0:W], wl[kc].ap[:, c0:c0 + 128], rhsT[kc].ap, start=(kc == 0), stop=(kc == 7)),
                                [wl[kc], rhsT[kc]], [ps], signal=(kc == 7))
                    P.op("act", lambda e, sa=sa, psGA=psGA: e.activation(sa.ap, psGA.ap[:, 0:W], AF.Sigmoid), [psGA], [sa])
                    P.op("act", lambda e, sb=sb, psGB=psGB: e.activation(sb.ap, psGB.ap[:, 0:W], AF.Sigmoid), [psGB], [sb])
                    P.op("dve", lambda e, sa=sa, psA=psA: e.tensor_tensor(sa.ap, sa.ap, psA.ap[:, 0:W], ALU.mult), [sa, psA], [sa])
                    P.op("dve", lambda e, sb=sb, psB=psB: e.tensor_tensor(sb.ap, sb.ap, psB.ap[:, 0:W], ALU.mult), [sb, psB], [sb])
                    P.op("pool", lambda e, ft=ft, sa=sa, sb=sb: e.tensor_tensor(mixT[ft].ap, sa.ap, sb.ap, ALU.add), [sa, sb], [mixT[ft]])
                for s in range(NS):
                    for n in range(2):
                        ps = pss[cnt % 6]
                        cnt += 1
                        for kc in range(8):
                            P.op("pe", lambda e, kc=kc, ps=ps, s=s, n=n: e.matmul(
                                ps.ap, mixT[kc].ap[:, s * 128:(s + 1) * 128], wo[kc].ap[:, n * 512:(n + 1) * 512],
                                start=(kc == 0), stop=(kc == 7)), [mixT[kc], wo[kc]], [ps], signal=(kc == 7))
                        P.op("dve", lambda e, ps=ps, s=s, n=n: e.tensor_tensor(xt.ap[:, s, n * 512:(n + 1) * 512], ps.ap,
                                                                             xt.ap[:, s, n * 512:(n + 1) * 512], ALU.add), [ps, xt], [xt])
                P.store("sp", xs[tsl, :].rearrange("(s p) d -> p s d", p=128), xt)
            P.barrier()
            P.release()
            if stop_after == 'P3':
                P.emit()
                return nc
            x_src = xs

            P.mark()
            NS = 2
            W = NS * 128
            wgt = [P.alloc("wgt%d" % kc, [DFF], BF16) for kc in range(8)]
            wup = [P.alloc("wup%d" % kc, [DFF], BF16) for kc in range(8)]
            wd = [P.alloc("wd%d" % fc, [D], BF16) for fc in range(NFT)]
            stg = [P.alloc("stg0", [D], F32)] * 2
            for kc in range(8):
                P.load("pool", wgt[kc], w_gate[l, kc * 128:(kc + 1) * 128, :])
                P.load("pool", wup[kc], w_up[l, kc * 128:(kc + 1) * 128, :])
            for fc in range(NFT):
                sg_ = stg[fc % 2]
                P.load("sp", sg_, w_down[l, fc * 128:(fc + 1) * 128, :])
                P.op("pool", lambda e, fc=fc, sg_=sg_: e.tensor_tensor(wd[fc].ap, sg_.ap, g2bc.ap, ALU.mult), [sg_, g2bc], [wd[fc]])
            P.barrier()
            xt = P.alloc("xt", [NS, D], F32)
            junk = P.alloc("junk", [D], BF16)
            xn = P.alloc("xn", [NS, D], BF16)
            ssall = P.alloc("ssall", [40], F32)
            hT = [P.alloc("hT%d" % kc, [W], BF16) for kc in range(8)]
            actT = [P.alloc("actT%d" % fc, [W], BF16) for fc in range(NFT)]
            halo = P.alloc("halo", [NFT, 2], F32)
            Gbs = [P.alloc("Gb%d" % i, [W + 2], F32) for i in range(2)]
            accs = [P.alloc("acc%d" % i, [W], F32) for i in range(2)]
            sgs = [P.alloc("sg%d" % i, [W], F32) for i in range(2)]
            P.op("pool", lambda e: e.memset(halo.ap, 0.0), [], [halo])
            tps = [P.psum_res(0, "tpb%d" % i, BF16, i * 512, (i + 1) * 512) for i in range(2)]
            pss = [P.psum_res(1 + i // 2, "ps%d" % i, F32, (i % 2) * 512, (i % 2 + 1) * 512) for i in range(6)]
            cnt = 0
            cw = lambda t, ft: vp.ap[:, vpo + VP_CW + t * NFT + ft: vpo + VP_CW + t * NFT + ft + 1]
            cb = lambda ft: vp.ap[:, vpo + VP_CB + ft: vpo + VP_CB + ft + 1]
            for tb in range(T // W):
                tsl = slice(tb * W, (tb + 1) * W)
                P.load("sp", xt, xs[tsl, :].rearrange("(s p) d -> p s d", p=128))
                norm_block(xt, ssall, 0, NS)
                rstd_from(ssall, NS, 1.0 / D, EPS)
                P.op("act", lambda e: e.activation(ssall.ap[:, 0:NS], ssall.ap[:, 0:NS], AF.Sqrt), [ssall], [ssall])
                P.op("dve", lambda e: e.reciprocal(ssall.ap[:, 0:NS], ssall.ap[:, 0:NS]), [ssall], [ssall])
                make_hT(xt, ssall, 0, hT, a2, modp.ap[:, 24:32], tps, NS)
                for ft in range(NFT):
                    psG, psU = pss[cnt % 6], pss[(cnt + 1) % 6]
                    cnt += 2
                    Gb, acc, sg = Gbs[ft % 2], accs[ft % 2], sgs[ft % 2]
                    for (ps, wl) in ((psG, wgt), (psU, wup)):
                        for kc in range(8):
                            P.op("pe", lambda e, kc=kc, ps=ps, wl=wl, ft=ft: e.matmul(
                                ps.ap[:, 0:W], wl[kc].ap[:, ft * 128:(ft + 1) * 128], hT[kc].ap, start=(kc == 0), stop=(kc == 7)),
                                [wl[kc], hT[kc]], [ps], signal=(kc == 7))
                    P.op("act", lambda e, Gb=Gb, psG=psG: e.copy(Gb.ap[:, 2:W + 2], psG.ap[:, 0:W]), [psG], [Gb])
                    P.op("pool", lambda e, Gb=Gb, ft=ft: e.tensor_copy(Gb.ap[:, 0:2], halo.ap[:, ft, :]), [halo], [Gb])
                    P.op("pool", lambda e, Gb=Gb, ft=ft: e.tensor_copy(halo.ap[:, ft, :], Gb.ap[:, W:W + 2]), [Gb], [halo])
                    P.op("dve", lambda e, Gb=Gb, acc=acc, ft=ft: e.tensor_scalar(acc.ap, Gb.ap[:, 2:W + 2], cw(2, ft), cb(ft), ALU.mult, ALU.add),
                         [Gb, vp], [acc])
                    P.op("dve", lambda e, Gb=Gb, acc=acc, ft=ft: e.scalar_tensor_tensor(acc.ap, Gb.ap[:, 1:W + 1], cw(1, ft), acc.ap, ALU.mult, ALU.add),
                         [Gb, vp, acc], [acc])
                    P.op("dve", lambda e, Gb=Gb, acc=acc, ft=ft: e.scalar_tensor_tensor(acc.ap, Gb.ap[:, 0:W], cw(0, ft), acc.ap, ALU.mult, ALU.add),
                         [Gb, vp, acc], [acc])
                    P.op("act", lambda e, sg=sg, acc=acc: e.activation(sg.ap, acc.ap, AF.Silu), [acc], [sg])
                    P.op("dve", lambda e, sg=sg, psU=psU, ft=ft: e.tensor_tensor(actT[ft].ap, sg.ap, psU.ap[:, 0:W], ALU.mult), [sg, psU], [actT[ft]])
                for s in range(NS):
                    for n in range(2):
                        ps = pss[cnt % 6]
                        cnt += 1
                        for fc in range(NFT):
                            P.op("pe", lambda e, fc=fc, ps=ps, s=s, n=n: e.matmul(
                                ps.ap, actT[fc].ap[:, s * 128:(s + 1) * 128], wd[fc].ap[:, n * 512:(n + 1) * 512],
                                start=(fc == 0), stop=(fc == NFT - 1)), [actT[fc], wd[fc]], [ps], signal=(fc == NFT - 1))
                        P.op("dve", lambda e, ps=ps, s=s, n=n: e.tensor_tensor(xt.ap[:, s, n * 512:(n + 1) * 512], ps.ap,
                                                                             xt.ap[:, s, n * 512:(n + 1) * 512], ALU.add), [ps, xt], [xt])
                if last:
                    norm_block(xt, ssall, 0, NS)
                    rstd_from(ssall, NS, 1.0 / D, EPS)
                    P.op("act", lambda e: e.activation(ssall.ap[:, 0:NS], ssall.ap[:, 0:NS], AF.Sqrt), [ssall], [ssall])
                    P.op("dve", lambda e: e.reciprocal(ssall.ap[:, 0:NS], ssall.ap[:, 0:NS]), [ssall], [ssall])
                    for s in range(NS):
                        P.op("dve", lambda e, s=s: e.scalar_tensor_tensor(xt.ap[:, s, :], xt.ap[:, s, :], ssall.ap[:, s:s + 1], fnbc.ap,
                                                                          ALU.mult, ALU.mult), [xt, ssall, fnbc], [xt])
                    P.store("sp", out[tsl, :].rearrange("(s p) d -> p s d", p=128), xt)
                else:
                    P.store("sp", xs[tsl, :].rearrange("(s p) d -> p s d", p=128), xt)
            P.barrier()
            P.release()
            if stop_after == 'P4':
                P.emit()
                return nc
        P.emit()
    return nc


def _consts():
    c = np.zeros((128, CC_N), np.float32)
    c[:, CC_ID:CC_ID + 128] = np.eye(128, dtype=np.float32)
    f = np.arange(128)
    partner = np.where(f % 64 < 32, f + 32, f - 32)
    perm = np.zeros((128, 128), np.float32)
    perm[partner, f] = 1.0
    c[:, CC_PERM:CC_PERM + 128] = perm
    k = np.arange(128)[:, None]
    q = np.arange(128)[None, :]
    tri = (k <= q).astype(np.float32)
    sup = (k > q).astype(np.float32)
    c[:, CC_TRI:CC_TRI + 512] = np.tile(tri, (1, 4))
    c[:, CC_SUP:CC_SUP + 512] = np.tile(sup, (1, 4))
    inv_freq = (10000.0 ** (-np.arange(0, 64, 2, dtype=np.float32) / np.float32(64))).astype(np.float32)
    c[:, CC_INVF] = inv_freq[f % 32]
    c[:, CC_SIGN] = np.where(f % 64 < 32, -1.0, 1.0)
    return c


def _fop(v, n):
    return np.ascontiguousarray(np.asarray(v, np.float32).reshape(n, 128).T)


def prep_inputs(b, T, x, c, positions, w_ada, b_ada, attn_norm, w_in, lambda_qk, subln_norm, sinks,
                w_branch_a, w_branch_b, w_out, ffn_norm, w_gate, w_up, conv_w, conv_b, w_down, final_norm):
    Lh = w_ada.shape[0]
    vecp = np.zeros((128, L_ALL * VP_L), np.float32)
    vecr = np.zeros((128, VR_N), np.float32)
    for l in range(Lh):
        o = l * VP_L
        vecp[:, o + VP_AN:o + VP_AN + 8] = _fop(attn_norm[l], 8)
        vecp[:, o + VP_FN:o + VP_FN + 8] = _fop(ffn_norm[l], 8)
        vecp[:, o + VP_BA:o + VP_BA + 48] = _fop(b_ada[l], 48)
        for t in range(3):
            vecp[:, o + VP_CW + t * NFT:o + VP_CW + (t + 1) * NFT] = _fop(conv_w[l, t], NFT)
        vecp[:, o + VP_CB:o + VP_CB + NFT] = _fop(conv_b[l], NFT)
        vecr[:, VR_BG + l * 2048:VR_BG + l * 2048 + 1024] = b_ada[l, 2 * D:3 * D][None, :]
        vecr[:, VR_BG + l * 2048 + 1024:VR_BG + (l + 1) * 2048] = b_ada[l, 5 * D:6 * D][None, :]
        vecr[:, VR_SUB + l * 128:VR_SUB + (l + 1) * 128] = subln_norm[l][None, :]
        vecr[:, VR_SNK + l * 16:VR_SNK + (l + 1) * 16] = sinks[l][None, :]
        vecr[:, VR_LQ + l * 256:VR_LQ + (l + 1) * 256] = lambda_qk[l].reshape(-1)[None, :]
    vecr[:, VR_FIN:VR_FIN + D] = final_norm[None, :]
    return {
        "x_in": np.ascontiguousarray(x[b, :T]),
        "cvec": _fop(c[b], 8),
        "posr": np.ascontiguousarray(np.broadcast_to(positions[b, :T][None, :], (128, T))).astype(np.int32),
        "vecp": vecp, "vecr": vecr, "consts": _consts(),
        "w_ada": w_ada, "w_in": w_in, "w_ba": w_branch_a, "w_bb": w_branch_b, "w_out": w_out,
        "w_gate": w_gate, "w_up": w_up, "w_down": w_down,
    }


def kernel(**inputs):
    inp = {k: np.asarray(v) for k, v in inputs.items()}
    B, T = inp["x"].shape[0], inp["x"].shape[1]
    nc = build_program(T, list(range(L_ALL)))
    in_maps = [prep_inputs(b, T, **inp) for b in range(B)]
    res = run_bass_kernel_spmd(nc, in_maps, core_ids=list(range(B)))
    return np.stack([np.asarray(r["out"], np.float32) for r in res.results], axis=0)
```

```python
import numpy as np
import ml_dtypes
from contextlib import ExitStack
import concourse.bass as bass
import concourse.mybir as mybir
from concourse.bass_utils import run_bass_kernel_spmd

F32 = mybir.dt.float32
BF16 = mybir.dt.bfloat16
I32 = mybir.dt.int32
AF = mybir.ActivationFunctionType
ALU = mybir.AluOpType
AX = mybir.AxisListType


class Res:
    __slots__ = ("ap", "w", "r", "dsem", "dcount", "name", "banks")

    def __init__(self, ap, name, banks=()):
        self.ap = ap
        self.name = name
        self.banks = banks
        self.w = None
        self.r = {}
        self.dsem = None
        self.dcount = 0


class Eng:
    def __init__(self, name, sem, self_sync):
        self.name = name
        self.sem = sem
        self.count = 0
        self.known = {}
        self.items = []
        self.self_sync = self_sync


class _Rec:
    def __init__(self):
        self.call = None

    def __getattr__(self, name):
        def f(*a, **k):
            self.call = (name, a, k)
            return self
        return f


class Prog:
    def __init__(self, nc, stack, arena_words, n_dsem=90):
        self.nc = nc
        self.stack = stack
        self.eng = {}
        for n in ("pe", "act", "dve", "pool", "sp"):
            sem = stack.enter_context(nc.semaphore("s_" + n))
            self.eng[n] = Eng(n, sem, n in ("act", "dve", "pool"))
        self.arena = stack.enter_context(nc.sbuf_tensor("arena", [128, arena_words], F32))
        self.arena_words = arena_words
        self.top = 0
        self.marks = []
        self.psum = [stack.enter_context(nc.psum_tensor("psb%d" % i, [128, 1024], F32)) for i in range(4)]
        self.dsem_pool = [stack.enter_context(nc.semaphore("d%d" % i)) for i in range(n_dsem)]
        self.free_dsems = [(sm, 0) for sm in self.dsem_pool]
        self.dres = []
        self.scopes = [[]]
        self.bank = [Res(None, "bank%d" % i) for i in range(8)]
        self.dummy = self.alloc("dummy", [8], F32)

    def alloc(self, name, free_shape, dtype, parts=128):
        esz = 2 if dtype == BF16 else 4
        n = int(np.prod(free_shape))
        words = (n * esz + 3) // 4
        words = (words + 7) // 8 * 8
        assert self.top + words <= self.arena_words, (name, self.top, words, self.arena_words)
        ap = self.arena[0:parts, self.top:self.top + words]
        self.top += words
        if dtype != F32:
            ap = ap.bitcast(dtype)
        ap = ap[:, 0:n]
        if len(free_shape) == 2:
            ap = ap.rearrange("p (a b) -> p a b", b=free_shape[1])
        elif len(free_shape) == 3:
            ap = ap.rearrange("p (a b c) -> p a b c", b=free_shape[1], c=free_shape[2])
        r = Res(ap, name)
        self.scopes[-1].append(r)
        return r

    def mark(self):
        self.marks.append(self.top)
        self.scopes.append([])

    def release(self):
        self.top = self.marks.pop()
        for r in self.scopes.pop():
            if r.dsem is not None:
                self.free_dsems.append((r.dsem, r.dcount))
                self.dres.remove(r)
                r.dsem = None

    def psum_res(self, bank2, name, dtype=F32, lo=0, hi=1024):
        ap = self.psum[bank2][:, lo:hi]
        if dtype != F32:
            ap = ap.bitcast(dtype)
        banks = tuple(range(2 * bank2 + lo // 512, 2 * bank2 + (hi - 1) // 512 + 1))
        return Res(ap, name, banks)

    def _filter(self, E, waits):
        out = {}
        for (s, v) in waits:
            if s is E.sem and not E.self_sync:
                continue
            k = id(s)
            if E.known.get(k, 0) >= v:
                continue
            if k not in out or out[k][1] < v:
                out[k] = (s, v)
        for k, (s, v) in out.items():
            E.known[k] = v
        return list(out.values())

    @staticmethod
    def _addr(res, ev):
        k = id(ev[0])
        if k not in res.r or res.r[k][1] < ev[1]:
            res.r[k] = ev

    def op(self, en, fn, reads=(), writes=(), signal=True):
        E = self.eng[en]
        pb = [self.bank[b] for r in list(reads) + list(writes) for b in r.banks]
        reads = [r for r in reads if not r.banks]
        writes = [w for w in writes if not w.banks] + pb
        waits = []
        for r in reads:
            if r.w is not None:
                waits.append(r.w)
        for w in writes:
            if w.w is not None:
                waits.append(w.w)
            waits.extend(w.r.values())
        if signal:
            E.count += 1
            ev = (E.sem, E.count)
        else:
            ev = (E.sem, E.count + 1)
        rec = _Rec()
        fn(rec)
        E.items.append((self._filter(E, waits), rec.call, True if signal else None))
        for r in reads:
            self._addr(r, ev)
        for w in writes:
            w.w = ev
            w.r = {}

    def dma(self, qn, out_ap, in_ap, res, write, **kw):
        Q = self.eng[qn]
        if res.dsem is None:
            res.dsem, res.dcount = self.free_dsems.pop(0)
            self.dres.append(res)
        waits = []
        if res.w is not None:
            waits.append(res.w)
        if write:
            waits.extend(res.r.values())
        res.dcount += 16
        ev = (res.dsem, res.dcount)
        Q.items.append((self._filter(Q, waits), ("dma_start", (), dict(out=out_ap, in_=in_ap, **kw)), res.dsem))
        if write:
            res.w = ev
            res.r = {}
        else:
            self._addr(res, ev)

    def load(self, qn, res, dram_ap, sub=None, **kw):
        self.dma(qn, res.ap if sub is None else sub, dram_ap, res, True, **kw)

    def store(self, qn, dram_ap, res, sub=None, **kw):
        self.dma(qn, dram_ap, res.ap if sub is None else sub, res, False, **kw)

    def barrier(self):
        D = self.eng["dve"]
        waits = []
        for E in self.eng.values():
            if E.count > 0:
                waits.append((E.sem, E.count))
        for r in self.dres:
            waits.append((r.dsem, r.dcount))
        D.count += 1
        dm = self.dummy.ap
        D.items.append((self._filter(D, waits), ("memset", (dm, 0.0), {}), True))
        ev = (D.sem, D.count)
        for n, E in self.eng.items():
            if n == "dve":
                continue
            w = self._filter(E, [ev])
            if w:
                E.items.append((w, None, None))

    def emit(self, final_waits_engine="sp"):
        nc = self.nc
        E = self.eng[final_waits_engine]
        waits = []
        for X in self.eng.values():
            if X.count > 0:
                waits.append((X.sem, X.count))
        for r in self.dres:
            waits.append((r.dsem, r.dcount))
        w = self._filter(E, waits)
        if w:
            E.items.append((w, None, None))

        def replay(e, E):
            for waits, fn, sig in E.items:
                for (s, v) in waits:
                    e.wait_ge(s, v)
                if fn is not None:
                    ins = getattr(e, fn[0])(*fn[1], **fn[2])
                    if sig is True:
                        ins.then_inc(E.sem, 1)
                    elif sig is not None:
                        ins.then_inc(sig, 16)

        with nc.Block() as block:
            @block.tensor
            def _(e):
                replay(e, self.eng["pe"])

            @block.scalar
            def _(e):
                replay(e, self.eng["act"])

            @block.vector
            def _(e):
                replay(e, self.eng["dve"])

            @block.gpsimd
            def _(e):
                replay(e, self.eng["pool"])

            @block.sync
            def _(e):
                replay(e, self.eng["sp"])


D = 1024
DFF = 2816
NFT = DFF // 128
L_ALL = 4
EPS = 1e-6
SUBLN_EPS = 1e-5
TWO_PI = 2.0 * np.pi
C1 = 6.28125
C2 = TWO_PI - 6.28125
MAGIC = 12582912.0

VP_AN, VP_FN, VP_BA, VP_CW, VP_CB = 0, 8, 16, 64, 130
VP_L = 152
VR_BG = 0
VR_FIN = L_ALL * 2048
VR_SUB = VR_FIN + 1024
VR_SNK = VR_SUB + L_ALL * 128
VR_LQ = VR_SNK + L_ALL * 16
VR_N = VR_LQ + L_ALL * 256
CC_ID, CC_PERM, CC_TRI, CC_SUP, CC_INVF, CC_SIGN = 0, 128, 256, 768, 1280, 1281
CC_N = 1282


def lambda_init(l):
    import math
    return 0.8 - 0.6 * math.exp(-0.3 * l)


def build_program(T, layers, debug=False, stop_after=None):
    NB = T // 128
    NTB = T // 512
    nc = bass.Bass("TRN2", target_bir_lowering=False)
    dt_in = lambda name, shape, dt=F32: nc.dram_tensor(name, shape, dt, kind="ExternalInput").ap()
    okind = "ExternalOutput" if debug else "Internal"
    dt_sc = lambda name, shape, dt: nc.dram_tensor(name, shape, dt, kind=okind).ap()
    x_in = dt_in("x_in", [T, D])
    cvec = dt_in("cvec", [128, 8])
    posr = dt_in("posr", [128, T], I32)
    vecp = dt_in("vecp", [128, L_ALL * VP_L])
    vecr = dt_in("vecr", [128, VR_N])
    consts = dt_in("consts", [128, CC_N])
    w_ada = dt_in("w_ada", [L_ALL, D, 6 * D])
    w_in = dt_in("w_in", [L_ALL, D, 6656])
    w_ba = dt_in("w_ba", [L_ALL, D, D])
    w_bb = dt_in("w_bb", [L_ALL, D, D])
    w_out = dt_in("w_out", [L_ALL, D, D])
    w_gate = dt_in("w_gate", [L_ALL, D, DFF])
    w_up = dt_in("w_up", [L_ALL, D, DFF])
    w_down = dt_in("w_down", [L_ALL, DFF, D])
    out = nc.dram_tensor("out", [T, D], F32, kind="ExternalOutput").ap()
    xs = dt_sc("xs", [T, D], F32)
    cosT = dt_sc("cosT", [128, T], F32)
    sinT = dt_sc("sinT", [128, T], F32)
    qT_da = dt_sc("qT_da", [8, 128, T], BF16)
    kT_da = dt_sc("kT_da", [8, 128, T], BF16)
    v_da = dt_sc("v_da", [T, D], BF16)
    qT_sw = dt_sc("qT_sw", [8, 128, T], BF16)
    kT_sw = dt_sc("kT_sw", [2, 128, T], BF16)
    v_sw = dt_sc("v_sw", [T, 256], BF16)
    oa = dt_sc("oa", [T, D], BF16)
    ob = dt_sc("ob", [T, D], BF16)

    with ExitStack() as st:
        P = Prog(nc, st, 51200)
        idb = P.alloc("idb", [128], BF16)
        permb = P.alloc("permb", [128], BF16)
        trib = P.alloc("trib", [512], BF16)
        supb = P.alloc("supb", [512], BF16)
        vp = P.alloc("vp", [L_ALL * VP_L], F32)
        cact = P.alloc("cact", [8], F32)
        onesf = P.alloc("onesf", [128], F32)
        modp = P.alloc("modp", [48], F32)
        a1 = P.alloc("a1", [8], F32)
        a2 = P.alloc("a2", [8], F32)
        g1bc = P.alloc("g1bc", [D], F32)
        g2bc = P.alloc("g2bc", [D], F32)
        lamt = P.alloc("lamt", [4], F32)
        gvec = P.alloc("gvec", [128], F32)
        esink = P.alloc("esink", [16], F32)
        fnbc = P.alloc("fnbc", [D], F32)

        P.mark()
        cst = P.alloc("cst", [CC_N], F32)
        P.load("sp", cst, consts)
        P.load("sp", vp, vecp)
        P.load("sp", cact, cvec)
        P.load("sp", fnbc, vecr[:, VR_FIN:VR_FIN + D])
        P.op("dve", lambda e: e.tensor_copy(idb.ap, cst.ap[:, CC_ID:CC_ID + 128]), [cst], [idb])
        P.op("dve", lambda e: e.tensor_copy(permb.ap, cst.ap[:, CC_PERM:CC_PERM + 128]), [cst], [permb])
        P.op("dve", lambda e: e.tensor_copy(trib.ap, cst.ap[:, CC_TRI:CC_TRI + 512]), [cst], [trib])
        P.op("dve", lambda e: e.tensor_copy(supb.ap, cst.ap[:, CC_SUP:CC_SUP + 512]), [cst], [supb])
        P.op("act", lambda e: e.activation(cact.ap, cact.ap, AF.Silu), [cact], [cact])
        P.op("dve", lambda e: e.memset(onesf.ap, 1.0), [], [onesf])
        invf = cst.ap[:, CC_INVF:CC_INVF + 1]
        sgn = cst.ap[:, CC_SIGN:CC_SIGN + 1]

        CH = min(T, 2048)
        pi_t = P.alloc("pi_t", [CH], I32)
        ang = P.alloc("ang", [CH], F32)
        kk = P.alloc("kk", [CH], F32)
        rr = P.alloc("rr", [CH], F32)
        so = P.alloc("so", [CH], F32)
        co = P.alloc("co", [CH], F32)
        for ch in range(T // CH):
            sl = slice(ch * CH, (ch + 1) * CH)
            P.load("sp", pi_t, posr[:, sl])
            P.op("dve", lambda e: e.tensor_copy(ang.ap, pi_t.ap), [pi_t], [ang])
            P.op("dve", lambda e: e.tensor_scalar(ang.ap, ang.ap, invf, None, ALU.mult), [ang, cst], [ang])
            P.op("dve", lambda e: e.tensor_scalar(kk.ap, ang.ap, 1.0 / TWO_PI, MAGIC, ALU.mult, ALU.add), [ang], [kk])
            P.op("dve", lambda e: e.tensor_scalar(kk.ap, kk.ap, -MAGIC, None, ALU.add), [kk], [kk])
            P.op("dve", lambda e: e.scalar_tensor_tensor(rr.ap, kk.ap, -C1, ang.ap, ALU.mult, ALU.add), [kk, ang], [rr])
            P.op("dve", lambda e: e.scalar_tensor_tensor(rr.ap, kk.ap, -C2, rr.ap, ALU.mult, ALU.add), [kk, rr], [rr])
            P.op("dve", lambda e: e.tensor_scalar(rr.ap, rr.ap, float(np.pi), float(-np.pi), ALU.min, ALU.max), [rr], [rr])
            P.op("act", lambda e: e.activation(so.ap, rr.ap, AF.Sin, scale=sgn), [rr, cst], [so])
            P.op("dve", lambda e: e.tensor_scalar(kk.ap, rr.ap, -1.0, None, ALU.mult), [rr], [kk])
            P.op("dve", lambda e: e.tensor_tensor(kk.ap, kk.ap, rr.ap, ALU.max), [kk, rr], [kk])
            P.op("dve", lambda e: e.tensor_scalar(kk.ap, kk.ap, -1.0, float(np.pi / 2), ALU.mult, ALU.add), [kk], [kk])
            P.op("act", lambda e: e.activation(co.ap, kk.ap, AF.Sin), [kk], [co])
            P.store("sp", sinT[:, sl], so)
            P.store("sp", cosT[:, sl], co)
        P.barrier()
        P.release()
        if stop_after == 'P0':
            P.emit()
            return nc

        x_src = x_in
        for li, l in enumerate(layers):
            last = (li == len(layers) - 1)
            vpo = l * VP_L
            linit = lambda_init(l)
            P.mark()
            wad = [P.alloc("wad%d" % i, [8, 1024], F32) for i in range(2)]
            bgr = P.alloc("bgr", [2048], F32)
            lqr = P.alloc("lqr", [256], F32)
            tmpv = P.alloc("tmpv", [128], F32)
            cbc = P.alloc("cbc", [8, 128], F32)
            for kc in range(8):
                P.op("dve", lambda e, kc=kc: e.tensor_scalar(cbc.ap[:, kc, :], onesf.ap, cact.ap[:, kc:kc + 1], None, ALU.mult),
                     [onesf, cact], [cbc])
            P.load("sp", bgr, vecr[:, VR_BG + l * 2048: VR_BG + (l + 1) * 2048])
            P.load("sp", lqr, vecr[:, VR_LQ + l * 256: VR_LQ + (l + 1) * 256])
            P.load("sp", gvec, vecr[:, VR_SUB + l * 128: VR_SUB + (l + 1) * 128])
            P.load("sp", esink, vecr[:, VR_SNK + l * 16: VR_SNK + (l + 1) * 16])
            P.op("dve", lambda e: e.tensor_scalar(gvec.ap, gvec.ap, float(1.0 - linit), None, ALU.mult), [gvec], [gvec])
            P.op("act", lambda e: e.activation(esink.ap, esink.ap, AF.Exp), [esink], [esink])
            for i in range(2):
                P.op("dve", lambda e, i=i: e.tensor_tensor(tmpv.ap[:, 0:64], lqr.ap[:, i * 128:i * 128 + 64],
                                                          lqr.ap[:, i * 128 + 64:i * 128 + 128], ALU.mult), [lqr], [tmpv])
                P.op("dve", lambda e, i=i: e.reduce_sum(lamt.ap[:, i:i + 1], tmpv.ap[:, 0:64], axis=AX.X), [tmpv], [lamt])
            P.op("act", lambda e: e.activation(lamt.ap[:, 0:2], lamt.ap[:, 0:2], AF.Exp), [lamt], [lamt])
            P.op("dve", lambda e: e.scalar_tensor_tensor(lamt.ap[:, 2:3], lamt.ap[:, 1:2], float(-linit), lamt.ap[:, 0:1],
                                                         ALU.add, ALU.subtract), [lamt], [lamt])
            nlam = lamt.ap[:, 2:3]
            mps = P.psum_res(0, "mps", F32, 0, 48)
            gpsl = [P.psum_res(1, "gps%d" % n, F32, n * 512, (n + 1) * 512) for n in range(2)]
            for g in range(6):
                wt = wad[g % 2]
                P.load("sp", wt, w_ada[l, :, g * 1024:(g + 1) * 1024].rearrange("(kc p) n -> p kc n", p=128))
                if g in (2, 5):
                    gdst = g1bc if g == 2 else g2bc
                    boff = 0 if g == 2 else 1024
                    for n in range(2):
                        gps = gpsl[n]
                        for kc in range(8):
                            P.op("pe", lambda e, kc=kc, n=n, gps=gps, wt=wt: e.matmul(
                                gps.ap, cbc.ap[:, kc, :], wt.ap[:, kc, n * 512:(n + 1) * 512],
                                start=(kc == 0), stop=(kc == 7)), [cbc, wt], [gps], signal=(kc == 7))
                        P.op("dve", lambda e, n=n, gps=gps, gdst=gdst, boff=boff: e.tensor_tensor(
                            gdst.ap[:, n * 512:(n + 1) * 512], gps.ap, bgr.ap[:, boff + n * 512: boff + (n + 1) * 512], ALU.add),
                            [gps, bgr], [gdst])
                else:
                    for j in range(8):
                        for kc in range(8):
                            P.op("pe", lambda e, kc=kc, j=j, g=g, wt=wt: e.matmul(
                                mps.ap[:, g * 8 + j: g * 8 + j + 1], wt.ap[:, kc, j * 128:(j + 1) * 128],
                                cact.ap[:, kc:kc + 1], start=(kc == 0), stop=(kc == 7)),
                                [wt, cact], [mps], signal=(kc == 7 and j == 7))
            for (m0, m1) in ((0, 16), (24, 40)):
                P.op("dve", lambda e, m0=m0, m1=m1: e.tensor_tensor(modp.ap[:, m0:m1], mps.ap[:, m0:m1],
                                                                  vp.ap[:, vpo + VP_BA + m0: vpo + VP_BA + m1], ALU.add),
                     [mps, vp], [modp])
            P.op("dve", lambda e: e.scalar_tensor_tensor(a1.ap, modp.ap[:, 8:16], 1.0, vp.ap[:, vpo + VP_AN: vpo + VP_AN + 8],
                                                         ALU.add, ALU.mult), [modp, vp], [a1])
            P.op("dve", lambda e: e.scalar_tensor_tensor(a2.ap, modp.ap[:, 32:40], 1.0, vp.ap[:, vpo + VP_FN: vpo + VP_FN + 8],
                                                         ALU.add, ALU.mult), [modp, vp], [a2])
            P.barrier()
            P.release()
            if stop_after == 'PL':
                P.emit()
                return nc

            xn_default = [None]

            def norm_block(xt, ssall, col0, ns=4, jres=None):
                for s in range(ns):
                    jr = junk if jres is None else jres
                    jap = junk.ap if jres is None else jres.ap[:, s, :]
                    P.op("act", lambda e, s=s: e.activation(jap, xt.ap[:, s, :], AF.Square,
                                                            accum_out=ssall.ap[:, col0 + s: col0 + s + 1]),
                         [xt], [jr, ssall])

            def rstd_from(ssall, n, scale, eps):
                P.op("dve", lambda e: e.tensor_scalar(ssall.ap[:, 0:n], ssall.ap[:, 0:n], scale, eps, ALU.mult, ALU.add),
                     [ssall], [ssall])

            def make_hT(xt, ssall, col0, hT, avec, svec, tb2, ns=4, xn_=None):
                xn = xn_ if xn_ is not None else xn_default[0]
                for s in range(ns):
                    P.op("dve", lambda e, s=s: e.tensor_scalar(xn.ap[:, s, :], xt.ap[:, s, :], ssall.ap[:, col0 + s: col0 + s + 1],
                                                               None, ALU.mult), [xt, ssall], [xn])
                for m in range(4):
                    tp = tb2[m % 2]
                    for k2 in range(2):
                        kc = 2 * m + k2
                        for s in range(ns):
                            P.op("pe", lambda e, s=s, kc=kc, k2=k2, tp=tp: e.transpose(
                                tp.ap[:, k2 * 512 + s * 128: k2 * 512 + (s + 1) * 128], xn.ap[:, s, kc * 128:(kc + 1) * 128], idb.ap),
                                [xn, idb], [tp], signal=(k2 == 1 and s == ns - 1))
                    for k2 in range(2):
                        kc = 2 * m + k2
                        P.op("dve", lambda e, kc=kc, k2=k2, tp=tp: e.tensor_scalar(
                            hT[kc].ap, tp.ap[:, k2 * 512: k2 * 512 + ns * 128], avec.ap[:, kc:kc + 1], svec[:, kc:kc + 1], ALU.mult, ALU.add),
                            [tp, avec, modp], [hT[kc]])

            P.mark()
            win = [P.alloc("win%d" % kc, [4608], BF16) for kc in range(8)]
            for kc in range(8):
                P.load("pool", win[kc], w_in[l, kc * 128:(kc + 1) * 128, 0:4608])
            xts = [P.alloc("xt%d" % i, [4, D], F32) for i in range(2)]
            junk = P.alloc("junk", [D], BF16)
            xn = P.alloc("xn", [4, D], BF16)
            xn_default[0] = xn
            ssall = P.alloc("ssall", [40], F32)
            hT = [P.alloc("hT%d" % kc, [512], BF16) for kc in range(8)]
            cs_t = P.alloc("cs_t", [512], F32)
            sn_t = P.alloc("sn_t", [512], F32)
            qbs = [P.alloc("qb%d" % i, [512], BF16) for i in range(2)]
            t1s = [P.alloc("t1%d" % i, [512], F32) for i in range(2)]
            t2s = [P.alloc("t2%d" % i, [512], F32) for i in range(2)]
            rss = [P.alloc("rs%d" % i, [512], BF16) for i in range(3)]
            vt = P.alloc("vt", [4, 1280], BF16)
            tps = [P.psum_res(0, "tpb%d" % i, BF16, i * 512, (i + 1) * 512) for i in range(2)]
            pss = [P.psum_res(1 + i // 2, "ps%d" % i, F32, (i % 2) * 512, (i % 2 + 1) * 512) for i in range(4)]
            rps = [P.psum_res(3, "rp%d" % i, F32, i * 512, (i + 1) * 512) for i in range(2)]
            tiles = ([("qda", i, i * 128) for i in range(8)] + [("kda", i, 1024 + i * 128) for i in range(8)] +
                     [("qsw", i, 3072 + i * 128) for i in range(8)] + [("ksw", i, 4096 + i * 128) for i in range(2)])
            dsts = {"qda": qT_da, "kda": kT_da, "qsw": qT_sw, "ksw": kT_sw}
            cnt = 0
            P.load("sp", xts[0], x_src[0:512, :].rearrange("(s p) d -> p s d", p=128))
            for tb in range(NTB):
                xt = xts[tb % 2]
                tsl = slice(tb * 512, (tb + 1) * 512)
                if tb + 1 < NTB:
                    P.load("sp", xts[(tb + 1) % 2], x_src[(tb + 1) * 512:(tb + 2) * 512, :].rearrange("(s p) d -> p s d", p=128))
                P.load("sp", cs_t, cosT[:, tsl])
                P.load("sp", sn_t, sinT[:, tsl])
                norm_block(xt, ssall, 0)
                rstd_from(ssall, 4, 1.0 / D, EPS)
                P.op("act", lambda e: e.activation(ssall.ap[:, 0:4], ssall.ap[:, 0:4], AF.Sqrt), [ssall], [ssall])
                P.op("dve", lambda e: e.reciprocal(ssall.ap[:, 0:4], ssall.ap[:, 0:4]), [ssall], [ssall])
                make_hT(xt, ssall, 0, hT, a1, modp.ap[:, 0:8], tps)
                def mm_tile(ti, cnt_):
                    c0_ = tiles[ti][2]
                    ps_ = pss[cnt_ % 4]
                    for kc in range(8):
                        P.op("pe", lambda e, kc=kc: e.matmul(ps_.ap, win[kc].ap[:, c0_:c0_ + 128], hT[kc].ap,
                                                             start=(kc == 0), stop=(kc == 7)),
                             [win[kc], hT[kc]], [ps_], signal=(kc == 7))
                mm_tile(0, cnt)
                for ti, (fam, idx, c0) in enumerate(tiles):
                    ps = pss[cnt % 4]
                    rp = rps[cnt % 2]
                    qb = qbs[cnt % 2]
                    t1 = t1s[cnt % 2]
                    t2 = t2s[cnt % 2]
                    rs = rss[cnt % 3]
                    cnt += 1
                    if ti + 1 < len(tiles):
                        mm_tile(ti + 1, cnt)
                    P.op("act", lambda e: e.copy(qb.ap, ps.ap), [ps], [qb])
                    P.op("pe", lambda e: e.matmul(rp.ap, permb.ap, qb.ap, start=True, stop=True), [permb, qb], [rp])
                    P.op("dve", lambda e: e.tensor_tensor(t1.ap, ps.ap, cs_t.ap, ALU.mult), [ps, cs_t], [t1])
                    P.op("dve", lambda e: e.tensor_tensor(t2.ap, rp.ap, sn_t.ap, ALU.mult), [rp, sn_t], [t2])
                    P.op("pool", lambda e: e.tensor_tensor(rs.ap, t1.ap, t2.ap, ALU.add), [t1, t2], [rs])
                    P.store("sp", dsts[fam][idx, :, tsl], rs)
                for s in range(4):
                    for (c0, wdt, o0) in ((2048, 512, 0), (2560, 512, 512), (4352, 256, 1024)):
                        ps = pss[cnt % 4]
                        cnt += 1
                        for kc in range(8):
                            P.op("pe", lambda e, kc=kc, ps=ps, c0=c0, wdt=wdt, s=s: e.matmul(
                                ps.ap[:, 0:wdt], hT[kc].ap[:, s * 128:(s + 1) * 128], win[kc].ap[:, c0:c0 + wdt],
                                start=(kc == 0), stop=(kc == 7)), [win[kc], hT[kc]], [ps], signal=(kc == 7))
                        P.op("act", lambda e, ps=ps, wdt=wdt, o0=o0, s=s: e.copy(vt.ap[:, s, o0:o0 + wdt], ps.ap[:, 0:wdt]), [ps], [vt])
                P.store("sp", v_da[tsl, :].rearrange("(s p) d -> p s d", p=128), vt, sub=vt.ap[:, :, 0:1024])
                P.store("sp", v_sw[tsl, :].rearrange("(s p) d -> p s d", p=128), vt, sub=vt.ap[:, :, 1024:1280])
            P.barrier()
            P.release()
            if stop_after == 'P1':
                P.emit()
                return nc

            P.mark()
            SC = min(T, 1024)
            NQB = SC // 128
            QS = P.alloc("QS", [4, SC], BF16)
            KS = P.alloc("KS", [128 + SC], BF16)
            VS = P.alloc("VS", [NQB + 1, 2, 65], BF16)
            pts = [P.alloc("spt%d" % i, [2, 512], BF16) for i in range(2)]
            obt = P.alloc("obt", [NQB, 2, 256], BF16)
            den = P.alloc("den", [4], F32)
            P.op("pool", lambda e: e.memset(VS.ap, 1.0), [], [VS])
            P.op("pool", lambda e: e.memset(KS.ap, 0.0), [], [KS])
            sps = [P.psum_res(i, "sps%d" % i, F32) for i in range(2)]
            ops_ = [P.psum_res(2, "sop%d" % i, F32, i * 512, i * 512 + 260) for i in range(2)]
            pts.append(P.alloc("spt2", [2, 512], BF16))
            for ch in range(T // SC):
                for kp in range(2):
                    t0 = ch * SC
                    for hh in range(2):
                        g = 2 * kp + hh
                        for j in range(4):
                            P.load("sp", QS, qT_sw[2 * g + j // 2, (j % 2) * 64:(j % 2) * 64 + 64, t0:t0 + SC],
                                   sub=QS.ap[hh * 64:(hh + 1) * 64, j, :])
                    if ch > 0:
                        P.load("sp", KS, kT_sw[kp, :, t0 - 128:t0 + SC])
                        for hh in range(2):
                            P.load("sp", VS, v_sw[t0 - 128:t0 + SC, kp * 128 + hh * 64:kp * 128 + (hh + 1) * 64].rearrange("(b p) d -> p b d", p=128),
                                   sub=VS.ap[:, :, hh, 0:64])
                    else:
                        P.load("sp", KS, kT_sw[kp, :, t0:t0 + SC], sub=KS.ap[:, 128:128 + SC])
                        for hh in range(2):
                            P.load("sp", VS, v_sw[t0:t0 + SC, kp * 128 + hh * 64:kp * 128 + (hh + 1) * 64].rearrange("(b p) d -> p b d", p=128),
                                   sub=VS.ap[:, 1:NQB + 1, hh, 0:64])
                    units = [(hh, i) for hh in range(2) for i in range(NQB)]

                    def sw_qk(u):
                        hh_, i_ = units[u]
                        gi_ = ch * NQB + i_
                        sp__ = sps[u % 2]
                        pl_ = slice(hh_ * 64, (hh_ + 1) * 64)
                        kbs_ = ([0] if gi_ > 0 else []) + [1]
                        for jj in kbs_:
                            P.op("pe", lambda e, jj=jj: e.matmul(
                                sp__.ap[:, jj * 512:(jj + 1) * 512], KS.ap[pl_, (i_ + jj) * 128:(i_ + jj + 1) * 128],
                                QS.ap[pl_, :, i_ * 128:(i_ + 1) * 128], start=True, stop=True), [KS, QS], [sp__])
                    sw_qk(0)
                    for u, (hh, i) in enumerate(units):
                        g = 2 * kp + hh
                        gi = ch * NQB + i
                        sp_ = sps[u % 2]
                        pt = pts[u % 3]
                        op_ = ops_[u % 2]
                        kbs = ([0] if gi > 0 else []) + [1]
                        if u + 1 < len(units):
                            sw_qk(u + 1)
                        lo = kbs[0] * 512
                        P.op("act", lambda e: e.activation(
                            pt.ap.rearrange("p a b -> p (a b)")[:, lo:1024], sp_.ap[:, lo:1024], AF.Exp, scale=0.125),
                            [sp_], [pt])
                        for jj in kbs:
                            msk = supb if jj == 0 else trib
                            P.op("pool", lambda e, jj=jj, msk=msk: e.tensor_tensor(pt.ap[:, jj, :], pt.ap[:, jj, :], msk.ap, ALU.mult),
                                 [pt, msk], [pt])
                        for j in range(4):
                            for n_, jj in enumerate(kbs):
                                P.op("pe", lambda e, j=j, jj=jj, n_=n_: e.matmul(
                                    op_.ap[:, j * 65:(j + 1) * 65], pt.ap[:, jj, j * 128:(j + 1) * 128], VS.ap[:, i + jj, hh, :],
                                    start=(n_ == 0), stop=(n_ == len(kbs) - 1)), [pt, VS], [op_],
                                    signal=(n_ == len(kbs) - 1))
                        opv = op_.ap.rearrange("p (j c) -> p j c", c=65)
                        P.op("dve", lambda e: e.tensor_tensor(den.ap, opv[:, :, 64], esink.ap[:, g * 4:(g + 1) * 4], ALU.add),
                             [op_, esink], [den])
                        P.op("dve", lambda e: e.reciprocal(den.ap, den.ap), [den], [den])
                        for j in range(4):
                            P.op("dve", lambda e, j=j: e.tensor_scalar(
                                obt.ap[:, i, hh, j * 64:(j + 1) * 64], opv[:, j, 0:64], den.ap[:, j:j + 1], None, ALU.mult),
                                [op_, den], [obt])
                    P.store("sp", ob[t0:t0 + SC, kp * 512:(kp + 1) * 512].rearrange("(b p) f -> p b f", p=128), obt,
                            sub=obt.ap.rearrange("p b h d -> p b (h d)"))
            P.barrier()
            P.release()
            if stop_after == 'SW':
                P.emit()
                return nc
            P.mark()
            KTs = [P.alloc("KT%d" % i, [T], BF16) for i in range(2)]
            QTs = [P.alloc("QT%d" % i, [2, T], BF16) for i in range(2)]
            for i in range(2):
                P.op("pool", lambda e, i=i: e.memset(QTs[i].ap, 0.0), [], [QTs[i]])
            Vs = [P.alloc("V%d" % i, [NB, 129], BF16) for i in range(2)]
            dpts = [P.alloc("dpt%d" % i, [1024], BF16) for i in range(4)]
            o0 = P.alloc("o0", [4, 129], F32)
            ofin = P.alloc("ofin", [4, 128], BF16)
            rcp = P.alloc("rcp", [4], F32)
            rcp8 = P.alloc("rcp8", [4], F32)
            tda = P.alloc("tda", [128], F32)
            for i in range(2):
                P.op("pool", lambda e, i=i: e.memset(Vs[i].ap, 1.0), [], [Vs[i]])
            dsp = [P.psum_res(i, "dsp%d" % i, F32) for i in range(2)]
            dop = [P.psum_res(2 + q // 2, "dop%d" % q, F32, (q % 2) * 512, (q % 2) * 512 + 129) for q in range(4)]
            pcnt = 0

            def load_head(h):
                P.load("sp", KTs[h % 2], kT_da[h])
                for c_ in range(2):
                    P.load("sp", QTs[h % 2], qT_da[h, c_ * 64:(c_ + 1) * 64, :], sub=QTs[h % 2].ap[c_ * 64:(c_ + 1) * 64, c_, :])
                P.load("sp", Vs[h % 2], v_da[:, h * 128:(h + 1) * 128].rearrange("(b p) e -> p b e", p=128),
                       sub=Vs[h % 2].ap[:, :, 0:128])
            load_head(0)
            o1 = P.alloc("o1", [4, 129], F32)
            pairs = []
            for h in range(8):
                for qg in range(NTB):
                    for c in range(2):
                        nkb = 4 * qg + 4
                        for j0 in range(0, nkb, 2):
                            pairs.append((h, qg, c, j0, j0 == 0 and c == 0 and qg == 0, j0 + 2 >= nkb))

            def emit_qk(i):
                h, qg, c, j0, _, _ = pairs[i]
                KT, QT = KTs[h % 2], QTs[h % 2]
                sp_ = dsp[i % 2]
                pl = slice(c * 64, (c + 1) * 64)
                for jj in range(2):
                    j = j0 + jj
                    col0 = max(j - 4 * qg, 0) * 128
                    P.op("pe", lambda e, jj=jj, j=j, col0=col0: e.matmul(
                        sp_.ap[:, jj * 512 + col0:(jj + 1) * 512], KT.ap[:, j * 128:(j + 1) * 128],
                        QT.ap[:, c, qg * 512 + col0:(qg + 1) * 512], start=True, stop=True),
                        [KT, QT], [sp_], signal=(jj == 1))

            emit_qk(0)
            for i, (h, qg, c, j0, first_of_head, last_of_unit) in enumerate(pairs):
                if first_of_head and h + 1 < 8:
                    load_head(h + 1)
                if i + 1 < len(pairs):
                    emit_qk(i + 1)
                V = Vs[h % 2]
                sp_ = dsp[i % 2]
                pt = dpts[i % 4]
                rs_ = [max(j0 + jj - 4 * qg, 0) for jj in range(2)]
                if rs_[0] == 0 and rs_[1] == 0:
                    P.op("act", lambda e: e.activation(pt.ap, sp_.ap, AF.Exp, scale=0.125), [sp_], [pt])
                else:
                    for jj in range(2):
                        a0 = jj * 512 + rs_[jj] * 128
                        a1_ = (jj + 1) * 512
                        P.op("act", lambda e, a0=a0, a1_=a1_: e.activation(
                            pt.ap[:, a0:a1_], sp_.ap[:, a0:a1_], AF.Exp, scale=0.125), [sp_], [pt])
                for jj in range(2):
                    j = j0 + jj
                    if j >= 4 * qg:
                        a0 = jj * 512 + (j - 4 * qg) * 128
                        P.op("pool", lambda e, a0=a0: e.tensor_tensor(pt.ap[:, a0:a0 + 128], pt.ap[:, a0:a0 + 128],
                                                                    trib.ap[:, 0:128], ALU.mult), [pt, trib], [pt])
                for jj in range(2):
                    j = j0 + jj
                    for qb in range(rs_[jj], 4):
                        stp = (j == 4 * qg + qb)
                        P.op("pe", lambda e, jj=jj, j=j, qb=qb, stp=stp: e.matmul(
                            dop[qb].ap, pt.ap[:, jj * 512 + qb * 128: jj * 512 + (qb + 1) * 128], V.ap[:, j, :],
                            start=(j == 0), stop=stp), [pt, V], [dop[qb]], signal=stp)
                if last_of_unit:
                    if c == 0:
                        for qb in range(4):
                            P.op("dve", lambda e, qb=qb: e.tensor_copy(o0.ap[:, qb, :], dop[qb].ap), [dop[qb]], [o0])
                    else:
                        for qb in range(4):
                            P.op("dve", lambda e, qb=qb: e.tensor_copy(o1.ap[:, qb, :], dop[qb].ap), [dop[qb]], [o1])
                        P.op("dve", lambda e: e.reciprocal(rcp.ap[:, 0:4], o0.ap[:, :, 128]), [o0], [rcp])
                        P.op("dve", lambda e: e.reciprocal(rcp8.ap[:, 0:4], o1.ap[:, :, 128]), [o1], [rcp8])
                        P.op("dve", lambda e: e.tensor_scalar(rcp8.ap[:, 0:4], rcp8.ap[:, 0:4], nlam, None, ALU.mult), [rcp8, lamt], [rcp8])
                        for qb in range(4):
                            P.op("dve", lambda e, qb=qb: e.tensor_scalar(tda.ap, o0.ap[:, qb, 0:128], rcp.ap[:, qb:qb + 1], None, ALU.mult),
                                 [o0, rcp], [tda])
                            P.op("dve", lambda e, qb=qb: e.scalar_tensor_tensor(ofin.ap[:, qb, :], o1.ap[:, qb, 0:128], rcp8.ap[:, qb:qb + 1],
                                                                                tda.ap, ALU.mult, ALU.add), [o1, rcp8, tda], [ofin])
                        P.store("sp", oa[qg * 512:(qg + 1) * 512, h * 128:(h + 1) * 128].rearrange("(s p) e -> p s e", p=128), ofin)
            P.barrier()
            P.release()
            if stop_after == 'DA':
                P.emit()
                return nc

            P.mark()
            NS = 2
            W = NS * 128
            wg = [P.alloc("wg%d" % kc, [2048], BF16) for kc in range(8)]
            wa = [P.alloc("wa%d" % kc, [D], BF16) for kc in range(8)]
            wb = [P.alloc("wb%d" % kc, [D], BF16) for kc in range(8)]
            wo = [P.alloc("wo%d" % kc, [D], BF16) for kc in range(8)]
            stg = [P.alloc("stg%d" % i, [D], F32) for i in range(2)]
            for kc in range(8):
                P.load("pool", wg[kc], w_in[l, kc * 128:(kc + 1) * 128, 4608:6656])
                P.load("pool", wa[kc], w_ba[l, kc * 128:(kc + 1) * 128, :])
                P.load("pool", wb[kc], w_bb[l, kc * 128:(kc + 1) * 128, :])
                sg_ = stg[kc % 2]
                P.load("sp", sg_, w_out[l, kc * 128:(kc + 1) * 128, :])
                P.op("pool", lambda e, kc=kc, sg_=sg_: e.tensor_tensor(wo[kc].ap, sg_.ap, g1bc.ap, ALU.mult), [sg_, g1bc], [wo[kc]])
            P.barrier()
            NB3 = T // W
            xts = [P.alloc("xt%d" % i, [NS, D], F32) for i in range(2)]
            oats = [P.alloc("oat%d" % i, [NS, D], BF16) for i in range(2)]
            obt3s = [P.alloc("obt3%d" % i, [NS, D], BF16) for i in range(2)]
            junkf = P.alloc("junkf", [D], F32)
            xns = [P.alloc("xn%d" % i, [NS, D], BF16) for i in range(2)]
            ssalls = [P.alloc("ssall%d" % i, [40], F32) for i in range(2)]
            hTs = [[P.alloc("hT%d_%d" % (i, kc), [W], BF16) for kc in range(8)] for i in range(2)]
            oaTs = [[P.alloc("oaT%d_%d" % (i, kc), [W], BF16) for kc in range(8)] for i in range(2)]
            obTs = [[P.alloc("obT%d_%d" % (i, kc), [W], BF16) for kc in range(8)] for i in range(2)]
            mixT = [P.alloc("mixT%d" % kc, [W], BF16) for kc in range(8)]
            sas = [P.alloc("sa%d" % i, [W], F32) for i in range(2)]
            sbs = [P.alloc("sb%d" % i, [W], F32) for i in range(2)]
            tps = [P.psum_res(0, "tpb%d" % i, BF16, i * 512, (i + 1) * 512) for i in range(2)]
            pss = [P.psum_res(1 + i // 2, "ps%d" % i, F32, (i % 2) * 512, (i % 2 + 1) * 512) for i in range(6)]
            cnt = 0

            def pre_load3(tb):
                b_ = tb % 2
                tsl_ = slice(tb * W, (tb + 1) * W)
                P.load("sp", xts[b_], x_src[tsl_, :].rearrange("(s p) d -> p s d", p=128))
                P.load("sp", oats[b_], oa[tsl_, :].rearrange("(s p) d -> p s d", p=128))
                P.load("sp", obt3s[b_], ob[tsl_, :].rearrange("(s p) d -> p s d", p=128))

            def pre_comp3(tb):
                b_ = tb % 2
                xt, oat, obt3, ssall, xn_b = xts[b_], oats[b_], obt3s[b_], ssalls[b_], xns[b_]
                norm_block(xt, ssall, 0, NS, jres=xn_b)
                for s in range(NS):
                    P.op("dve", lambda e, s=s: e.tensor_tensor(junkf.ap, oat.ap[:, s, :], oat.ap[:, s, :], ALU.mult), [oat], [junkf])
                    P.op("dve", lambda e, s=s: e.reduce_sum(ssall.ap[:, NS + s * 8: NS + 8 + s * 8],
                                                            junkf.ap.rearrange("p (h e) -> p h e", e=128), axis=AX.X),
                         [junkf], [ssall])
                P.op("dve", lambda e: e.tensor_scalar(ssall.ap[:, 0:NS], ssall.ap[:, 0:NS], 1.0 / D, EPS, ALU.mult, ALU.add), [ssall], [ssall])
                P.op("dve", lambda e: e.tensor_scalar(ssall.ap[:, NS:9 * NS], ssall.ap[:, NS:9 * NS], 1.0 / 128, SUBLN_EPS, ALU.mult, ALU.add), [ssall], [ssall])
                P.op("act", lambda e: e.activation(ssall.ap[:, 0:9 * NS], ssall.ap[:, 0:9 * NS], AF.Sqrt), [ssall], [ssall])
                P.op("dve", lambda e: e.reciprocal(ssall.ap[:, 0:9 * NS], ssall.ap[:, 0:9 * NS]), [ssall], [ssall])
                make_hT(xt, ssall, 0, hTs[b_], a1, modp.ap[:, 0:8], tps, NS, xn_=xn_b)
                for s in range(NS):
                    for h in range(8):
                        P.op("dve", lambda e, s=s, h=h: e.scalar_tensor_tensor(
                            oat.ap[:, s, h * 128:(h + 1) * 128], oat.ap[:, s, h * 128:(h + 1) * 128],
                            ssall.ap[:, NS + s * 8 + h: NS + 1 + s * 8 + h], gvec.ap, ALU.mult, ALU.mult), [oat, ssall, gvec], [oat])
                for (src, dstT) in ((oat, oaTs[b_]), (obt3, obTs[b_])):
                    for m in range(4):
                        tp = tps[m % 2]
                        for k2 in range(2):
                            kc = 2 * m + k2
                            for s in range(NS):
                                P.op("pe", lambda e, s=s, kc=kc, k2=k2: e.transpose(
                                    tp.ap[:, k2 * 512 + s * 128: k2 * 512 + (s + 1) * 128], src.ap[:, s, kc * 128:(kc + 1) * 128], idb.ap),
                                    [src, idb], [tp], signal=(k2 == 1 and s == NS - 1))
                        for k2 in range(2):
                            kc = 2 * m + k2
                            P.op("act", lambda e, kc=kc, k2=k2: e.copy(dstT[kc].ap, tp.ap[:, k2 * 512: k2 * 512 + W]),
                                 [tp], [dstT[kc]])

            pre_load3(0)
            pre_comp3(0)
            for tb in range(NB3):
                tsl = slice(tb * W, (tb + 1) * W)
                b_ = tb % 2
                xt, hT, oaT, obT = xts[b_], hTs[b_], oaTs[b_], obTs[b_]
                if tb + 1 < NB3:
                    pre_load3(tb + 1)
                for ft in range(8):
                    if ft == 4 and tb + 1 < NB3:
                        pre_comp3(tb + 1)
                    psA, psB, psGA, psGB = [pss[(cnt + i) % 6] for i in range(4)]
                    cnt += 4
                    sa, sb = sas[ft % 2], sbs[ft % 2]
                    for (ps, wl, c0, rhsT) in ((psA, wa, ft * 128, oaT), (psB, wb, ft * 128, obT),
                                               (psGA, wg, ft * 128, hT), (psGB, wg, 1024 + ft * 128, hT)):
                        for kc in range(8):
                            P.op("pe", lambda e, kc=kc: e.matmul(
                                ps.ap[:, 0:W], wl[kc].ap[:, c0:c0 + 128], rhsT[kc].ap, start=(kc == 0), stop=(kc == 7)),
                                [wl[kc], rhsT[kc]], [ps], signal=(kc == 7))
                    P.op("act", lambda e: e.activation(sa.ap, psGA.ap[:, 0:W], AF.Sigmoid), [psGA], [sa])
                    P.op("act", lambda e: e.activation(sb.ap, psGB.ap[:, 0:W], AF.Sigmoid), [psGB], [sb])
                    P.op("dve", lambda e: e.tensor_tensor(sa.ap, sa.ap, psA.ap[:, 0:W], ALU.mult), [sa, psA], [sa])
                    P.op("dve", lambda e: e.tensor_tensor(sb.ap, sb.ap, psB.ap[:, 0:W], ALU.mult), [sb, psB], [sb])
                    P.op("pool", lambda e: e.tensor_tensor(mixT[ft].ap, sa.ap, sb.ap, ALU.add), [sa, sb], [mixT[ft]])
                for s in range(NS):
                    for n in range(2):
                        ps = pss[cnt % 6]
                        cnt += 1
                        for kc in range(8):
                            P.op("pe", lambda e, kc=kc: e.matmul(
                                ps.ap, mixT[kc].ap[:, s * 128:(s + 1) * 128], wo[kc].ap[:, n * 512:(n + 1) * 512],
                                start=(kc == 0), stop=(kc == 7)), [mixT[kc], wo[kc]], [ps], signal=(kc == 7))
                        P.op("dve", lambda e: e.tensor_tensor(xt.ap[:, s, n * 512:(n + 1) * 512], ps.ap,
                                                              xt.ap[:, s, n * 512:(n + 1) * 512], ALU.add), [ps, xt], [xt])
                P.store("sp", xs[tsl, :].rearrange("(s p) d -> p s d", p=128), xt)
            P.barrier()
            P.release()
            if stop_after == 'P3':
                P.emit()
                return nc
            x_src = xs

            P.mark()
            NS = 2
            W = NS * 128
            NB4 = T // W
            wgt = [P.alloc("wgt%d" % kc, [DFF], BF16) for kc in range(8)]
            wup = [P.alloc("wup%d" % kc, [DFF], BF16) for kc in range(8)]
            wd = [P.alloc("wd%d" % fc, [D], BF16) for fc in range(NFT)]
            xts = [P.alloc("xt%d" % i, [NS, D], F32) for i in range(2)]
            for kc in range(8):
                P.load("pool", wgt[kc], w_gate[l, kc * 128:(kc + 1) * 128, :])
                P.load("pool", wup[kc], w_up[l, kc * 128:(kc + 1) * 128, :])
            for fc in range(NFT):
                sg_ = xts[fc % 2]
                P.load("sp", sg_, w_down[l, fc * 128:(fc + 1) * 128, :], sub=sg_.ap[:, 0, :])
                P.op("pool", lambda e, fc=fc, sg_=sg_: e.tensor_tensor(wd[fc].ap, sg_.ap[:, 0, :], g2bc.ap, ALU.mult), [sg_, g2bc], [wd[fc]])
            P.barrier()
            xns = [P.alloc("xn%d" % i, [NS, D], BF16) for i in range(2)]
            ssalls = [P.alloc("ssall%d" % i, [40], F32) for i in range(2)]
            hTs = [[P.alloc("hT%d_%d" % (i, kc), [W], BF16) for kc in range(8)] for i in range(2)]
            actT = [P.alloc("actT%d" % fc, [W], BF16) for fc in range(NFT)]
            halo = P.alloc("halo", [NFT, 2], F32)
            Gbs = [P.alloc("Gb%d" % i, [W + 2], F32) for i in range(2)]
            accs = [P.alloc("acc%d" % i, [W], F32) for i in range(2)]
            sgs = [P.alloc("sg%d" % i, [W], F32) for i in range(2)]
            P.op("pool", lambda e: e.memset(halo.ap, 0.0), [], [halo])
            tps = [P.psum_res(0, "tpb%d" % i, BF16, i * 512, (i + 1) * 512) for i in range(2)]
            pss = [P.psum_res(1 + i // 2, "ps%d" % i, F32, (i % 2) * 512, (i % 2 + 1) * 512) for i in range(6)]
            cnt = 0
            cw = lambda t, ft: vp.ap[:, vpo + VP_CW + t * NFT + ft: vpo + VP_CW + t * NFT + ft + 1]
            cb = lambda ft: vp.ap[:, vpo + VP_CB + ft: vpo + VP_CB + ft + 1]

            def rstd_small(ssall):
                rstd_from(ssall, NS, 1.0 / D, EPS)
                P.op("act", lambda e: e.activation(ssall.ap[:, 0:NS], ssall.ap[:, 0:NS], AF.Sqrt), [ssall], [ssall])
                P.op("dve", lambda e: e.reciprocal(ssall.ap[:, 0:NS], ssall.ap[:, 0:NS]), [ssall], [ssall])

            def pre_load4(tb):
                P.load("sp", xts[tb % 2], xs[tb * W:(tb + 1) * W, :].rearrange("(s p) d -> p s d", p=128))

            def pre_comp4(tb):
                b_ = tb % 2
                norm_block(xts[b_], ssalls[b_], 0, NS, jres=xns[b_])
                rstd_small(ssalls[b_])
                make_hT(xts[b_], ssalls[b_], 0, hTs[b_], a2, modp.ap[:, 24:32], tps, NS, xn_=xns[b_])

            pre_load4(0)
            pre_comp4(0)
            for tb in range(NB4):
                tsl = slice(tb * W, (tb + 1) * W)
                xt, ssall, hT, xn_b = xts[tb % 2], ssalls[tb % 2], hTs[tb % 2], xns[tb % 2]
                if tb + 1 < NB4:
                    pre_load4(tb + 1)
                for ft in range(NFT):
                    if ft == NFT // 2 and tb + 1 < NB4:
                        pre_comp4(tb + 1)
                    psG, psU = pss[cnt % 6], pss[(cnt + 1) % 6]
                    cnt += 2
                    Gb, acc, sg = Gbs[ft % 2], accs[ft % 2], sgs[ft % 2]
                    for (ps, wl) in ((psG, wgt), (psU, wup)):
                        for kc in range(8):
                            P.op("pe", lambda e, kc=kc, ps=ps, wl=wl: e.matmul(
                                ps.ap[:, 0:W], wl[kc].ap[:, ft * 128:(ft + 1) * 128], hT[kc].ap, start=(kc == 0), stop=(kc == 7)),
                                [wl[kc], hT[kc]], [ps], signal=(kc == 7))
                    P.op("act", lambda e: e.copy(Gb.ap[:, 2:W + 2], psG.ap[:, 0:W]), [psG], [Gb])
                    P.op("pool", lambda e: e.tensor_copy(Gb.ap[:, 0:2], halo.ap[:, ft, :]), [halo], [Gb])
                    P.op("pool", lambda e: e.tensor_copy(halo.ap[:, ft, :], Gb.ap[:, W:W + 2]), [Gb], [halo])
                    P.op("dve", lambda e: e.tensor_scalar(acc.ap, Gb.ap[:, 2:W + 2], cw(2, ft), cb(ft), ALU.mult, ALU.add),
                         [Gb, vp], [acc])
                    P.op("dve", lambda e: e.scalar_tensor_tensor(acc.ap, Gb.ap[:, 1:W + 1], cw(1, ft), acc.ap, ALU.mult, ALU.add),
                         [Gb, vp, acc], [acc])
                    P.op("dve", lambda e: e.scalar_tensor_tensor(acc.ap, Gb.ap[:, 0:W], cw(0, ft), acc.ap, ALU.mult, ALU.add),
                         [Gb, vp, acc], [acc])
                    P.op("act", lambda e: e.activation(sg.ap, acc.ap, AF.Silu), [acc], [sg])
                    P.op("dve", lambda e: e.tensor_tensor(actT[ft].ap, sg.ap, psU.ap[:, 0:W], ALU.mult), [sg, psU], [actT[ft]])
                for s in range(NS):
                    for n in range(2):
                        ps = pss[cnt % 6]
                        cnt += 1
                        for fc in range(NFT):
                            P.op("pe", lambda e, fc=fc: e.matmul(
                                ps.ap, actT[fc].ap[:, s * 128:(s + 1) * 128], wd[fc].ap[:, n * 512:(n + 1) * 512],
                                start=(fc == 0), stop=(fc == NFT - 1)), [actT[fc], wd[fc]], [ps], signal=(fc == NFT - 1))
                        P.op("dve", lambda e: e.tensor_tensor(xt.ap[:, s, n * 512:(n + 1) * 512], ps.ap,
                                                              xt.ap[:, s, n * 512:(n + 1) * 512], ALU.add), [ps, xt], [xt])
                if last:
                    norm_block(xt, ssall, 0, NS, jres=xn_b)
                    rstd_small(ssall)
                    for s in range(NS):
                        P.op("dve", lambda e, s=s: e.scalar_tensor_tensor(xt.ap[:, s, :], xt.ap[:, s, :], ssall.ap[:, s:s + 1], fnbc.ap,
                                                                          ALU.mult, ALU.mult), [xt, ssall, fnbc], [xt])
                    P.store("sp", out[tsl, :].rearrange("(s p) d -> p s d", p=128), xt)
                else:
                    P.store("sp", xs[tsl, :].rearrange("(s p) d -> p s d", p=128), xt)
            P.barrier()
            P.release()
            if stop_after == 'P4':
                P.emit()
                return nc
        P.emit()
    return nc


def _consts():
    c = np.zeros((128, CC_N), np.float32)
    c[:, CC_ID:CC_ID + 128] = np.eye(128, dtype=np.float32)
    f = np.arange(128)
    partner = np.where(f % 64 < 32, f + 32, f - 32)
    perm = np.zeros((128, 128), np.float32)
    perm[partner, f] = 1.0
    c[:, CC_PERM:CC_PERM + 128] = perm
    k = np.arange(128)[:, None]
    q = np.arange(128)[None, :]
    tri = (k <= q).astype(np.float32)
    sup = (k > q).astype(np.float32)
    c[:, CC_TRI:CC_TRI + 512] = np.tile(tri, (1, 4))
    c[:, CC_SUP:CC_SUP + 512] = np.tile(sup, (1, 4))
    inv_freq = (10000.0 ** (-np.arange(0, 64, 2, dtype=np.float32) / np.float32(64))).astype(np.float32)
    c[:, CC_INVF] = inv_freq[f % 32]
    c[:, CC_SIGN] = np.where(f % 64 < 32, -1.0, 1.0)
    return c


def _fop(v, n):
    return np.ascontiguousarray(np.asarray(v, np.float32).reshape(n, 128).T)


def prep_inputs(b, T, x, c, positions, w_ada, b_ada, attn_norm, w_in, lambda_qk, subln_norm, sinks,
                w_branch_a, w_branch_b, w_out, ffn_norm, w_gate, w_up, conv_w, conv_b, w_down, final_norm):
    Lh = w_ada.shape[0]
    vecp = np.zeros((128, L_ALL * VP_L), np.float32)
    vecr = np.zeros((128, VR_N), np.float32)
    for l in range(Lh):
        o = l * VP_L
        vecp[:, o + VP_AN:o + VP_AN + 8] = _fop(attn_norm[l], 8)
        vecp[:, o + VP_FN:o + VP_FN + 8] = _fop(ffn_norm[l], 8)
        vecp[:, o + VP_BA:o + VP_BA + 48] = _fop(b_ada[l], 48)
        for t in range(3):
            vecp[:, o + VP_CW + t * NFT:o + VP_CW + (t + 1) * NFT] = _fop(conv_w[l, t], NFT)
        vecp[:, o + VP_CB:o + VP_CB + NFT] = _fop(conv_b[l], NFT)
        vecr[:, VR_BG + l * 2048:VR_BG + l * 2048 + 1024] = b_ada[l, 2 * D:3 * D][None, :]
        vecr[:, VR_BG + l * 2048 + 1024:VR_BG + (l + 1) * 2048] = b_ada[l, 5 * D:6 * D][None, :]
        vecr[:, VR_SUB + l * 128:VR_SUB + (l + 1) * 128] = subln_norm[l][None, :]
        vecr[:, VR_SNK + l * 16:VR_SNK + (l + 1) * 16] = sinks[l][None, :]
        vecr[:, VR_LQ + l * 256:VR_LQ + (l + 1) * 256] = lambda_qk[l].reshape(-1)[None, :]
    vecr[:, VR_FIN:VR_FIN + D] = final_norm[None, :]
    return {
        "x_in": np.ascontiguousarray(x[b, :T]),
        "cvec": _fop(c[b], 8),
        "posr": np.ascontiguousarray(np.broadcast_to(positions[b, :T][None, :], (128, T))).astype(np.int32),
        "vecp": vecp, "vecr": vecr, "consts": _consts(),
        "w_ada": w_ada, "w_in": w_in, "w_ba": w_branch_a, "w_bb": w_branch_b, "w_out": w_out,
        "w_gate": w_gate, "w_up": w_up, "w_down": w_down,
    }


def kernel(**inputs):
    inp = {k: np.asarray(v) for k, v in inputs.items()}
    B, T = inp["x"].shape[0], inp["x"].shape[1]
    nc = build_program(T, list(range(L_ALL)))
    in_maps = [prep_inputs(b, T, **inp) for b in range(B)]
    res = run_bass_kernel_spmd(nc, in_maps, core_ids=list(range(B)))
    return np.stack([np.asarray(r["out"], np.float32) for r in res.results], axis=0)
```
